# Optimizing a Trainium2 kernel written in Bass

```python
import jax, jax.numpy as jnp
from jax import lax
import numpy as np

D_MODEL = 1024
BATCH = 8
SEQ = 2048
DEPTH = 2

CHUNK = 64
N_META = 16

SB_HEAD_DIM = 64
SB_WIDTH = D_MODEL // 2
N_SB_HEADS = SB_WIDTH // SB_HEAD_DIM
Q_BLOCK = 128

POOL_WINDOWS = (2, 4, 8, 16)
N_POOL_GROUPS = len(POOL_WINDOWS)
POOL_WIDTH = D_MODEL // 2
POOL_GROUP_DIM = POOL_WIDTH // N_POOL_GROUPS

N_BRANCHES = 2
N_IN = 3 * SB_WIDTH + POOL_WIDTH + N_BRANCHES * D_MODEL

D_FF = ((-(-8 * D_MODEL // 3) + 255) // 256) * 256
RMS_EPS = 1e-6

kernel_name = "hybrid_pool_stickbreak_meta"


def rms_norm(x, g):
    xf = x.astype(jnp.float32)
    y = xf * lax.rsqrt(jnp.mean(xf * xf, axis=-1, keepdims=True) + RMS_EPS)
    return (y * g.astype(jnp.float32)).astype(x.dtype)


def pool_mixer(u, mix, scale):
    b, l, _ = u.shape
    uf = u.astype(jnp.float32)
    cs0 = jnp.pad(jnp.cumsum(uf, axis=1), ((0, 0), (1, 0), (0, 0)))
    pos = jnp.arange(l)
    means = []
    for g, w in enumerate(POOL_WINDOWS):
        c = cs0[:, :, g * POOL_GROUP_DIM:(g + 1) * POOL_GROUP_DIM]
        cur = c[:, 1:]
        lag = jnp.pad(c, ((0, 0), (w - 1, 0), (0, 0)))[:, :l]
        cnt = jnp.minimum(pos + 1, w).astype(jnp.float32)[None, :, None]
        means.append((cur - lag) / cnt)
    pooled = jnp.stack(means, axis=2)
    diff = (pooled - uf.reshape(b, l, N_POOL_GROUPS, POOL_GROUP_DIM)).astype(u.dtype)
    mixed = jnp.einsum('blgc,gcd->blgd', diff, mix)
    return mixed.reshape(b, l, POOL_WIDTH) * scale


def stick_breaking_attention(q, k, v):
    b, l, h, dh = q.shape
    lp = -(-l // Q_BLOCK) * Q_BLOCK
    padw = ((0, 0), (0, lp - l), (0, 0), (0, 0))
    qh, kh, vh = [jnp.pad(a, padw).transpose(0, 2, 1, 3) for a in (q, k, v)]
    scale = 1.0 / float(np.sqrt(dh))
    outs = []
    for i in range(lp // Q_BLOCK):
        q0 = i * Q_BLOCK
        kend = q0 + Q_BLOCK
        qb = qh[:, :, q0:kend]
        kb = kh[:, :, :kend]
        vb = vh[:, :, :kend]
        z = jnp.einsum('bhqd,bhkd->bhqk', qb, kb).astype(jnp.float32) * scale
        t_idx = q0 + jnp.arange(Q_BLOCK)
        s_idx = jnp.arange(kend)
        mask = s_idx[None, :] < t_idx[:, None]
        log_keep = jnp.where(mask, jax.nn.log_sigmoid(-z), 0.0)
        log_stick = lax.cumsum(log_keep, axis=3, reverse=True) - log_keep
        w = jnp.where(mask, jnp.exp(jax.nn.log_sigmoid(z) + log_stick), 0.0)
        outs.append(jnp.einsum('bhqk,bhkd->bhqd', w.astype(v.dtype), vb))
    o = jnp.concatenate(outs, axis=2)[:, :, :l]
    return o.transpose(0, 2, 1, 3).reshape(b, l, h * dh)


def hybrid_layer(x, norm1_g, w_in, b_gate, pool_mix, pool_scale, w_branch_pool, w_branch_sb,
                 w_out, norm2_g, w_ffn_in, w_ffn_out):
    b, l, _ = x.shape
    h = rms_norm(x, norm1_g)
    proj = h @ w_in
    q, k, v, u, gate_pre = jnp.split(
        proj, [SB_WIDTH, 2 * SB_WIDTH, 3 * SB_WIDTH, 3 * SB_WIDTH + POOL_WIDTH], axis=-1)
    gates = jax.nn.sigmoid(gate_pre + b_gate)
    g_pool, g_sb = jnp.split(gates, N_BRANCHES, axis=-1)
    a = pool_mixer(u, pool_mix, pool_scale)
    hs = (b, l, N_SB_HEADS, SB_HEAD_DIM)
    s = stick_breaking_attention(q.reshape(hs), k.reshape(hs), v.reshape(hs))
    merged = g_pool * (a @ w_branch_pool) + g_sb * (s @ w_branch_sb)
    x = x + merged @ w_out
    h2 = rms_norm(x, norm2_g)
    gt, up = jnp.split(h2 @ w_ffn_in, 2, axis=-1)
    return x + (jax.nn.silu(gt) * up) @ w_ffn_out


def setup_inputs(seed: int = 0) -> dict:
    key = jax.random.key(seed)
    ks = jax.random.split(key, 16)
    f = jnp.float32
    nrm = lambda kk, shape, s: jax.random.normal(kk, shape, f) * s
    return {
        "x": nrm(ks[0], (BATCH, SEQ, D_MODEL), 1.0),
        "meta_tokens": nrm(ks[1], (N_META, D_MODEL), 1.0),
        "norm1_g": 1.0 + nrm(ks[2], (DEPTH, D_MODEL), 0.02),
        "w_in": nrm(ks[3], (DEPTH, D_MODEL, N_IN), D_MODEL ** -0.5),
        "b_gate": nrm(ks[4], (DEPTH, N_BRANCHES * D_MODEL), 0.02),
        "pool_mix": nrm(ks[5], (DEPTH, N_POOL_GROUPS, POOL_GROUP_DIM, POOL_GROUP_DIM), POOL_GROUP_DIM ** -0.5),
        "pool_scale": 1.0 + nrm(ks[6], (DEPTH, POOL_WIDTH), 0.02),
        "w_branch_pool": nrm(ks[7], (DEPTH, POOL_WIDTH, D_MODEL), POOL_WIDTH ** -0.5),
        "w_branch_sb": nrm(ks[8], (DEPTH, SB_WIDTH, D_MODEL), SB_WIDTH ** -0.5),
        "w_out": nrm(ks[9], (DEPTH, D_MODEL, D_MODEL), D_MODEL ** -0.5),
        "norm2_g": 1.0 + nrm(ks[10], (DEPTH, D_MODEL), 0.02),
        "w_ffn_in": nrm(ks[11], (DEPTH, D_MODEL, 2 * D_FF), D_MODEL ** -0.5),
        "w_ffn_out": nrm(ks[12], (DEPTH, D_FF, D_MODEL), D_FF ** -0.5),
        "final_norm_g": 1.0 + nrm(ks[13], (D_MODEL,), 0.02),
    }


def reference(x, meta_tokens, norm1_g, w_in, b_gate, pool_mix, pool_scale, w_branch_pool,
              w_branch_sb, w_out, norm2_g, w_ffn_in, w_ffn_out, final_norm_g):
    b = x.shape[0]
    meta = jnp.broadcast_to(meta_tokens[None].astype(x.dtype), (b, N_META, x.shape[-1]))
    hcat = jnp.concatenate([meta, x], axis=1)
    for layer in range(DEPTH):
        hcat = hybrid_layer(hcat, norm1_g[layer], w_in[layer], b_gate[layer], pool_mix[layer],
                            pool_scale[layer], w_branch_pool[layer], w_branch_sb[layer],
                            w_out[layer], norm2_g[layer], w_ffn_in[layer], w_ffn_out[layer])
    hcat = rms_norm(hcat, final_norm_g)
    return hcat[:, N_META:]
```

```python
import contextlib
import numpy as np
import concourse.bass as bass
import concourse.mybir as mybir
from concourse.bass_utils import run_bass_kernel_spmd

F32 = mybir.dt.float32
BF16 = mybir.dt.bfloat16
AF = mybir.ActivationFunctionType
ALU = mybir.AluOpType

D = 1024
SEQ = 2048
NMETA = 16
NT = SEQ + NMETA
DFF = 2816
DEPTH = 2
EPS = 1e-6
TT = [(0, 16)] + [(16 + 512 * i, 16 + 512 * (i + 1)) for i in range(4)]
KB = [(0, 16)] + [(16 + 128 * j, 144 + 128 * j) for j in range(16)]
CELL = 16
NEG = -30000.0
GATT = 5
FLAGS = dict(av_tp00=True, mask_after=True, pair_act=True)

PC_N1 = 0
PC_N2 = 16
PC_FN = 32
PC_BG = 40
PC_PS = 72
NPC = 80
NBAND = 1600
NCST = 384 + NBAND


class Prog:
    def __init__(self, nc):
        self.nc = nc
        self.ops = []
        self.cnt = {}
        self.arenas = {}

    def arena(self, name, nelem):
        self.arenas[name] = dict(n=(nelem + CELL - 1) // CELL, W={}, R={})

    def _record(self, eng, fn, reads, writes, stream, val, skip):
        need = {}
        for (an, lo, hi) in reads:
            A = self.arenas[an]
            c0, c1 = lo // CELL, (hi + CELL - 1) // CELL
            for s, arr in A['W'].items():
                v = int(arr[c0:c1].max())
                if v > need.get(s, 0):
                    need[s] = v
        for (an, lo, hi) in writes:
            A = self.arenas[an]
            c0, c1 = lo // CELL, (hi + CELL - 1) // CELL
            for tab in (A['W'], A['R']):
                for s, arr in tab.items():
                    v = int(arr[c0:c1].max())
                    if v > need.get(s, 0):
                        need[s] = v
        for s in skip:
            need.pop(s, None)
        for s, v in need.items():
            if s.startswith('E:'):
                assert v <= self.cnt.get(s, 0), ("dependency on a not-yet-issued inc", s, v)
        for (an, lo, hi) in reads:
            A = self.arenas[an]
            c0, c1 = lo // CELL, (hi + CELL - 1) // CELL
            if stream not in A['R']:
                A['R'][stream] = np.zeros(A['n'], np.int64)
            A['R'][stream][c0:c1] = val
        for (an, lo, hi) in writes:
            A = self.arenas[an]
            c0, c1 = lo // CELL, (hi + CELL - 1) // CELL
            if stream not in A['W']:
                A['W'][stream] = np.zeros(A['n'], np.int64)
            A['W'][stream][c0:c1] = val
        return need

    def op(self, eng, fn, reads=(), writes=(), inc=True):
        stream = 'E:' + eng
        val = self.cnt.get(stream, 0) + 1
        skip = (stream,) if eng == 'pe' else ()
        need = self._record(eng, fn, reads, writes, stream, val, skip)
        if inc:
            self.cnt[stream] = val
        self.ops.append(dict(eng=eng, fn=fn, need=need, inc=inc, dma=None))

    def dma(self, queue, fn, sem, reads=(), writes=()):
        stream = 'D:' + sem
        val = self.cnt.get(stream, 0) + 16
        need = self._record(queue, fn, reads, writes, stream, val, (stream,))
        self.cnt[stream] = val
        self.ops.append(dict(eng=queue, fn=fn, need=need, inc=True, dma=sem))

    def emit(self):
        nc = self.nc
        ops = self.ops
        sem_names = [s for s in self.cnt if self.cnt[s] > 0]
        with contextlib.ExitStack() as st:
            sems = {}
            for n in sem_names:
                sems[n] = st.enter_context(nc.semaphore(n.replace(':', '_')))
            block = st.enter_context(nc.Block())

            def run(ename, eng):
                waited = {}
                for o in ops:
                    if o['eng'] != ename:
                        continue
                    for s, v in o['need'].items():
                        if waited.get(s, 0) < v:
                            eng.wait_ge(sems[s], v)
                            waited[s] = v
                    if o['fn'] is None:
                        continue
                    ins = o['fn'](eng)
                    if o['dma'] is not None:
                        ins.then_inc(sems['D:' + o['dma']], 16)
                    elif o['inc']:
                        ins.then_inc(sems['E:' + ename], 1)

            used = set(o['eng'] for o in ops)
            if 'pe' in used:
                @block.tensor
                def _(e):
                    run('pe', e)
            if 'act' in used:
                @block.scalar
                def _(e):
                    run('act', e)
            if 'dve' in used:
                @block.vector
                def _(e):
                    run('dve', e)
            if 'pool' in used:
                @block.gpsimd
                def _(e):
                    run('pool', e)
            if 'sp' in used:
                @block.sync
                def _(e):
                    run('sp', e)


class View:
    def __init__(self, tens, name, off, A, T):
        self.tens, self.name, self.off, self.A, self.T = tens, name, off, A, T

    def ap(self, a, t0, t1, p0=0, p1=128):
        b = self.off + a * self.T
        return self.tens[p0:p1, b + t0:b + t1]

    def r(self, a, t0, t1):
        b = self.off + a * self.T
        return (self.name, b + t0, b + t1)


def build(depth=DEPTH, dbg=None):
    nc = bass.Bass("TRN2", target_bir_lowering=False)
    dt = nc.dram_tensor
    xT_d = dt("xT", [D, SEQ], F32, kind="ExternalInput").ap()
    meta_d = dt("metaT", [D, NMETA], F32, kind="ExternalInput").ap()
    prm_d = dt("params", [128, NPC], F32, kind="ExternalInput").ap()
    cst_d = dt("consts", [128, NCST], F32, kind="ExternalInput").ap()
    w_in_d = dt("w_in", [DEPTH, D, 4096], F32, kind="ExternalInput").ap()
    pmix_d = dt("pool_mix", [DEPTH, 4, 128, 128], F32, kind="ExternalInput").ap()
    wbp_d = dt("w_branch_pool", [DEPTH, 512, D], F32, kind="ExternalInput").ap()
    wbs_d = dt("w_branch_sb", [DEPTH, 512, D], F32, kind="ExternalInput").ap()
    wout_d = dt("w_out", [DEPTH, D, D], F32, kind="ExternalInput").ap()
    wfi_d = dt("w_ffn_in", [DEPTH, D, 2 * DFF], F32, kind="ExternalInput").ap()
    wfo_d = dt("w_ffn_out", [DEPTH, DFF, D], F32, kind="ExternalInput").ap()
    y_d = dt("yT", [D, SEQ], F32, kind="ExternalOutput").ap()
    dbg_d = {}
    if dbg:
        for name, shape in dbg.items():
            dbg_d[name] = dt("dbg_" + name, shape, F32, kind="ExternalOutput").ap()

    P = Prog(nc)
    with contextlib.ExitStack() as st:
        def sb(name, n, dtype):
            P.arena(name, n)
            return st.enter_context(nc.sbuf_tensor(name, [128, n], dtype))

        Xt = sb("X", 8 * NT, F32)
        PRM = sb("PRM", NPC, F32)
        FW = sb("FW", 2048, F32)
        Ht = sb("H", 8 * NT, BF16)
        MIX = sb("MIX", 33472, BF16)
        WB = sb("WB", 4 * 2048, BF16)
        CST = sb("CST", 512 + NBAND, BF16)
        BW = sb("BW", 8192, BF16)
        ZER = sb("ZER", 128, BF16)
        PMX = sb("PMX", 512, BF16)
        PSB = [st.enter_context(nc.psum_tensor("psb%d" % b, [128, 1024], F32)) for b in range(4)]

        class _Bank:
            def __init__(self, b):
                self.t, self.o = PSB[b // 2], (b % 2) * 512

            def __getitem__(self, idx):
                ps_, cs_ = idx
                c0 = cs_.start or 0
                c1 = 512 if cs_.stop is None else cs_.stop
                return self.t[ps_, self.o + c0:self.o + c1]

        PS = [_Bank(b) for b in range(8)]
        P.arena('ps', 8 * CELL)
        P.arena('out', CELL)
        P.arena('dbg', CELL)

        X = View(Xt, "X", 0, 8, NT)
        H = View(Ht, "H", 0, 8, NT)
        Q = View(MIX, "MIX", 0, 4, NT)
        K = View(MIX, "MIX", 8256, 4, NT)
        V = View(MIX, "MIX", 16512, 17, 512)
        A = View(MIX, "MIX", 25216, 4, NT)
        MG = View(MIX, "MIX", 8256, 8, NT)

        def psr(b):
            return ('ps', b * CELL, (b + 1) * CELL)

        def fw(lo, hi):
            return FW[:, lo:hi], ('FW', lo, hi)

        def bw(lo, hi):
            return BW[:, lo:hi], ('BW', lo, hi)

        IDENT = CST[:, 0:128]
        NEGU = CST[:, 128:256]
        NEGM = CST[:, 256:384]
        NEGONES = CST[:, 384:512]
        r_cst = ('CST', 0, 512)

        psn = [0]
        ps_pool = [list(range(8))]

        def newps():
            pool_ = ps_pool[0]
            b = pool_[psn[0] % len(pool_)]
            psn[0] += 1
            return b

        def mm(out, lhsT, rhs, start, stop, reads, writes, inc, tp=None):
            if tp is None:
                P.op('pe', lambda e: e.matmul(out, lhsT=lhsT, rhs=rhs, start=start, stop=stop),
                     reads, writes, inc)
            else:
                P.op('pe', lambda e: e.matmul(out, lhsT=lhsT, rhs=rhs, start=start, stop=stop,
                                              tile_position=tp), reads, writes, inc)

        def mmx(out, lhsT, rhs, start, stop, reads, writes, inc):
            P.op('pe', lambda e: e.matmul(out, lhsT=lhsT, rhs=rhs, start=start, stop=stop,
                                          skip_group_check=True), reads, writes, inc)

        def act(out, in_, func, reads, writes, bias=None, scale=None):
            kw = {}
            if bias is not None:
                kw['bias'] = bias
            if scale is not None:
                kw['scale'] = scale
            P.op('act', lambda e: e.activation(out=out, in_=in_, func=func, **kw), reads, writes)

        def tt(eng, out, in0, in1, op, reads, writes):
            P.op(eng, lambda e: e.tensor_tensor(out=out, in0=in0, in1=in1, op=op), reads, writes)

        def stt(out, in0, scalar, in1, op0, op1, reads, writes):
            P.op('dve', lambda e: e.scalar_tensor_tensor(out=out, in0=in0, scalar=scalar, in1=in1,
                                                         op0=op0, op1=op1), reads, writes)

        def cp(eng, out, in_, reads, writes):
            if eng == 'act':
                P.op(eng, lambda e: e.copy(out=out, in_=in_), reads, writes)
            else:
                P.op(eng, lambda e: e.tensor_copy(out=out, in_=in_), reads, writes)

        def memset(eng, ap, val, writes):
            P.op(eng, lambda e: e.memset(ap, val), (), writes)

        def wdma(out, in_, sem, writes):
            P.dma('pool', lambda e: e.dma_start(out=out, in_=in_), sem, (), writes)

        P.dma('sp', lambda e: e.dma_start(out=PRM[:, :], in_=prm_d), 'prm', (), [('PRM', 0, NPC)])
        wdma(CST[:, 0:384], cst_d[:, 0:384], 'cst', [('CST', 0, 384)])
        wdma(CST[:, 512:512 + NBAND], cst_d[:, 384:NCST], 'cst2', [('CST', 512, 512 + NBAND)])
        memset('dve', CST[:, 384:512], -1.0, [('CST', 384, 512)])
        memset('dve', ZER[:, :], 0.0, [('ZER', 0, 128)])
        X3 = Xt[:, :].rearrange("p (k t) -> p k t", t=NT)
        P.dma('sp', lambda e: e.dma_start(out=X3[:, :, 0:16], in_=meta_d.rearrange("(k p) t -> p k t", p=128)),
              'x0', (), [X.r(kc, 0, 16) for kc in range(8)])
        for i in range(1, 5):
            t0, t1 = TT[i]
            src = xT_d[:, t0 - 16:t1 - 16].rearrange("(k p) t -> p k t", p=128)
            dst = X3[:, :, t0:t1]
            P.dma('sp', lambda e, dst=dst, src=src: e.dma_start(out=dst, in_=src),
                  'x%d' % i, (), [X.r(kc, t0, t1) for kc in range(8)])

        def wunit(u):
            if u < 4:
                return WB, 'WB', u * 2048
            return BW, 'BW', (u - 4) * 2048

        def load_in_slab(l, col0, u0):
            off = u0 * 2048
            dst = WB[:, off:off + 4096].rearrange("p (k n) -> p k n", n=512)
            src = w_in_d[l, :, col0:col0 + 512].rearrange("(k p) n -> p k n", p=128)
            wdma(dst, src, 'wu%d' % u0, [('WB', off, off + 4096)])

        def load_merge_slab(l, j):
            us = (0, 1, 2) if j % 2 == 0 else (3, 4, 5)
            for idx, col0 in ((0, 2048 + 256 * j), (1, 3072 + 256 * j)):
                T_, an, off = wunit(us[idx])
                dst = T_[:, off:off + 2048].rearrange("p (k n) -> p k n", n=256)
                src = w_in_d[l, :, col0:col0 + 256].rearrange("(k p) n -> p k n", p=128)
                wdma(dst, src, 'wu%d' % us[idx], [(an, off, off + 2048)])
            T_, an, off = wunit(us[2])
            for idx, wd in ((0, wbp_d), (1, wbs_d)):
                dst = T_[:, off + idx * 1024:off + idx * 1024 + 1024].rearrange("p (k n) -> p k n", n=256)
                src = wd[l, :, 256 * j:256 * j + 256].rearrange("(k p) n -> p k n", p=128)
                wdma(dst, src, 'wu%d' % us[2], [(an, off, off + 2048)])

        def load_out_slab(l, jj):
            off = jj * 4096
            dst = WB[:, off:off + 4096].rearrange("p (k n) -> p k n", n=512)
            src = wout_d[l, :, 512 * jj:512 * jj + 512].rearrange("(k p) n -> p k n", p=128)
            wdma(dst, src, 'wu%d' % (2 * jj), [('WB', off, off + 4096)])

        FFN_SLABS = [(0, 2), (2, 4), (6, 4), (10, 4), (14, 4), (18, 4)]

        def ffn_nfc(j):
            return FFN_SLABS[j][1]

        FFN_SLOT = [(0, 25216), (8256, 16448)]
        FFN_ACT = 29312

        def ffn_ranges(j):
            wib, wob = FFN_SLOT[j % 2]
            return [('MIX', wib, wib + 8192), ('MIX', wob, wob + 4096)]

        def load_ffn_slab(l, j):
            wib, wob = FFN_SLOT[j % 2]
            nfc = ffn_nfc(j)
            nc_ = 128 * nfc
            wi4 = MIX[:, wib:wib + 8192].rearrange("p (k g c) -> p k g c", k=8, g=2)
            sem = 'f%d' % (j % 2)
            for gu in range(2):
                fo = FFN_SLABS[j][0] * 128
                src = wfi_d[l, :, gu * DFF + fo:gu * DFF + fo + nc_].rearrange("(k p) n -> p k n", p=128)
                wdma(wi4[:, :, gu, 0:nc_], src, sem, ffn_ranges(j))
            dst = MIX[:, wob:wob + nfc * 1024].rearrange("p (f n) -> p f n", n=1024)
            fo = FFN_SLABS[j][0] * 128
            src = wfo_d[l, fo:fo + nc_, :].rearrange("(f p) n -> p f n", p=128)
            wdma(dst, src, sem, ffn_ranges(j))

        def load_pmx(l):
            dst = PMX[:, :].rearrange("p (g d) -> p g d", d=128)
            src = pmix_d[l].rearrange("g c d -> c g d")
            wdma(dst, src, 'pmx', [('PMX', 0, 512)])

        def norm_stats(i):
            t0, t1 = TT[i]
            N = t1 - t0
            pb = newps()
            for kc in range(8):
                sq, rsq = bw((kc % 2) * 512, (kc % 2) * 512 + N)
                act(sq, X.ap(kc, t0, t1), AF.Square, [X.r(kc, t0, t1)], [rsq])
                mm(PS[pb][:, 0:N], NEGONES, sq, kc == 0, kc == 7, [rsq, r_cst], [psr(pb)], True)
            lnv, rln = fw(0, N)
            rstd, rrs = fw(512, 512 + N)
            act(lnv, PS[pb][:, 0:N], AF.Ln, [psr(pb)], [rln], bias=EPS, scale=-1.0 / D)
            act(rstd, lnv, AF.Exp, [rln], [rrs], scale=-0.5)
            return rstd, rrs

        def norm_to_h(i, gcol):
            t0, t1 = TT[i]
            rstd, rrs = norm_stats(i)
            for kc in range(8):
                stt(H.ap(kc, t0, t1), X.ap(kc, t0, t1), PRM[:, gcol + kc:gcol + kc + 1], rstd,
                    ALU.mult, ALU.mult, [X.r(kc, t0, t1), rrs, ('PRM', 0, NPC)], [H.r(kc, t0, t1)])

        def tile_kbs(i):
            return [0] if i == 0 else list(range(4 * (i - 1) + 1, 4 * i + 1))

        r_band = ('CST', 512, 512 + NBAND)

        def proj_u(l, i, u0, ecnt):
            off = u0 * 2048
            rw = ('WB', off, off + 4096)
            for kb in tile_kbs(i):
                b0, b1 = KB[kb]
                nk = b1 - b0
                pb = newps()
                for kc in range(8):
                    mm(PS[pb][0:nk, 0:512], H.ap(kc, b0, b1), WB[:, off + kc * 512:off + kc * 512 + 512],
                       kc == 0, kc == 7, [rw, H.r(kc, b0, b1)], [psr(pb)], kc == 7)
                ub = 2048 + (kb % 6) * 512
                eng = 'act' if ecnt[0] % 2 == 0 else 'dve'
                ecnt[0] += 1
                cp(eng, BW[0:nk, ub:ub + 512], PS[pb][0:nk, 0:512], [psr(pb)], [('BW', ub, ub + 512)])

        def pool_band(l, i):
            t0, t1 = TT[i]
            N = t1 - t0
            kbs = tile_kbs(i)
            pend = []

            def band(g):
                pd = newps()
                gb = 512 + g * 400
                for j, kb in enumerate(kbs):
                    b0, b1 = KB[kb]
                    nk = b1 - b0
                    ub = 2048 + (kb % 6) * 512
                    c0 = j * 128
                    if kb == 0:
                        mmx(PS[pd][:, c0:c0 + nk], BW[0:16, ub + g * 128:ub + g * 128 + 128],
                            CST[0:16, gb + 256:gb + 272], True, True, [('BW', ub, ub + 512), r_band], [psr(pd)],
                            j == len(kbs) - 1)
                        continue
                    mmx(PS[pd][:, c0:c0 + nk], BW[:, ub + g * 128:ub + g * 128 + 128], CST[:, gb:gb + 128],
                        True, False, [('BW', ub, ub + 512), r_band], [psr(pd)], False)
                    pk = kb - 1
                    pub = 2048 + (pk % 6) * 512
                    if pk == 0:
                        mmx(PS[pd][:, c0:c0 + nk], BW[0:16, pub + g * 128:pub + g * 128 + 128],
                            CST[0:16, gb + 272:gb + 400], False, True, [('BW', pub, pub + 512), r_band],
                            [psr(pd)], j == len(kbs) - 1)
                    else:
                        mmx(PS[pd][:, c0:c0 + nk], BW[:, pub + g * 128:pub + g * 128 + 128],
                            CST[:, gb + 128:gb + 256], False, True, [('BW', pub, pub + 512), r_band],
                            [psr(pd)], j == len(kbs) - 1)
                db = 1024 + (g % 2) * 512
                cp('dve', BW[:, db:db + N], PS[pd][:, 0:N], [psr(pd)], [('BW', db, db + 512)])
                return db

            def mix(g, db):
                pb2 = newps()
                mm(PS[pb2][:, 0:N], PMX[:, g * 128:(g + 1) * 128], BW[:, db:db + N], True, True,
                   [('BW', db, db + 512), ('PMX', 0, 512)], [psr(pb2)], True)
                pc = PC_PS + l * 4 + g
                act(A.ap(g, t0, t1), PS[pb2][:, 0:N], AF.Identity, [psr(pb2), ('PRM', 0, NPC)], [A.r(g, t0, t1)],
                    scale=PRM[:, pc:pc + 1])

            d0 = band(0)
            d1 = band(1)
            mix(0, d0)
            d2 = band(2)
            mix(1, d1)
            d3 = band(3)
            mix(2, d2)
            mix(3, d3)

        def proj_fm_items(l, i, u0, kind, eng='act'):
            t0, t1 = TT[i]
            N = t1 - t0
            off = u0 * 2048
            rw = ('WB', off, off + 4096)
            items = []
            for n in range(4):
                def item(n=n):
                    pb = newps()
                    for kc in range(8):
                        mm(PS[pb][:, 0:N], WB[:, off + kc * 512 + n * 128:off + kc * 512 + n * 128 + 128],
                           H.ap(kc, t0, t1), kc == 0, kc == 7, [rw, H.r(kc, t0, t1)], [psr(pb)], kc == 7)
                    if kind == 'k':
                        cp(eng, K.ap(n, t0, t1), PS[pb][:, 0:N], [psr(pb)], [K.r(n, t0, t1)])
                    else:
                        act(Q.ap(n, t0, t1), PS[pb][:, 0:N], AF.Identity, [psr(pb)], [Q.r(n, t0, t1)], scale=0.125)
                items.append(item)
            return items

        def proj_fm(l, i, u0, kind):
            for it in proj_fm_items(l, i, u0, kind):
                it()

        def proj_v_items(l, i, u0):
            off = u0 * 2048
            rw = ('WB', off, off + 4096)
            items = []
            for kb in tile_kbs(i):
                def item(kb=kb):
                    b0, b1 = KB[kb]
                    nk = b1 - b0
                    pb = newps()
                    for kc in range(8):
                        mm(PS[pb][0:nk, 0:512], H.ap(kc, b0, b1), WB[:, off + kc * 512:off + kc * 512 + 512],
                           kc == 0, kc == 7, [rw, H.r(kc, b0, b1)], [psr(pb)], kc == 7)
                    cp('dve', V.ap(kb, 0, 512, 0, nk), PS[pb][0:nk, 0:512], [psr(pb)], [V.r(kb, 0, 512)])
                items.append(item)
            return items

        def proj_v(l, i, u0):
            for it in proj_v_items(l, i, u0):
                it()

        def attention(work=None, after_work=None, ft=0):
            PO = 6
            ps_pool[0] = [7]
            SB_S = GATT * 1024
            assert GATT <= 5
            zc = [0]
            wc = [0]
            for i in range(ft, 5):
                t0, t1 = TT[i]
                N = t1 - t0
                kbs = [0] if i == 0 else list(range(4 * i, -1, -1))
                first_diag = 0 if i == 0 else 4 * (i - 1) + 1
                n = len(kbs)

                def geom(si):
                    kb = kbs[si]
                    b0, b1 = KB[kb]
                    diag = kb >= first_diag
                    c0 = (kb - first_diag) * 128 if (diag and i > 0) else 0
                    return kb, b0, b1, b1 - b0, diag, c0

                def pair(T_, base, nk, c0):
                    return T_[0:nk, base:base + 1024].rearrange("p (h c) -> p h c", h=2)[:, :, c0:N]

                def qk(hp, si, r):
                    kb, b0, b1, nk, diag, c0 = geom(si)
                    dw = min(128, N - c0)
                    for hh in range(2):
                        pz = 2 * r + hh
                        p0, p1 = 64 * hh, 64 * hh + 64
                        mmx(PS[pz][0:nk, c0:N], K.ap(hp, b0, b1, p0, p1), Q.ap(hp, t0 + c0, t1, p0, p1),
                            True, not diag, [K.r(hp, b0, b1), Q.r(hp, t0 + c0, t1)], [psr(pz)], not diag)
                    if diag:
                        for hh in range(2):
                            pz = 2 * r + hh
                            mmx(PS[pz][0:nk, c0:c0 + dw], CST[0:nk, 0:nk], CST[0:nk, 256:256 + dw],
                                False, True, [r_cst], [psr(pz)], True)

                def prep(a):
                    kind, hp, si, gi = a['kind'], a['hp'], a['si'], a['gi']
                    kb, b0, b1, nk, diag, c0 = geom(si)
                    r = zc[0] % 3
                    zc[0] += 1
                    a['r'] = r
                    qk(hp, si, r)
                    if kind == 'ex':
                        sp_r = ('BW', gi * 1024, gi * 1024 + 1024)
                        so = SB_S + (si % 2) * 1024
                        sn = SB_S + ((si + 1) % 2) * 1024
                        s_r = ('BW', so, so + 1024)
                        if si == 0 and n > 1:
                            memset('dve', BW[:, SB_S:SB_S + 2048], 0.0, [('BW', SB_S, SB_S + 2048)])
                        for hh in range(2):
                            pz = 2 * r + hh
                            sp_ap = BW[0:nk, gi * 1024 + hh * 512 + c0:gi * 1024 + hh * 512 + N]
                            mmx(PS[pz][0:nk, c0:N], CST[0:nk, 128:128 + nk], sp_ap, False, si == 0,
                                [r_cst, sp_r], [psr(pz)], si == 0)
                            if si > 0:
                                mmx(PS[pz][0:nk, c0:N], CST[:, 384:384 + nk],
                                    BW[:, so + hh * 512 + c0:so + hh * 512 + N], False, True,
                                    [r_cst, s_r], [psr(pz)], True)
                        if si < n - 1:
                            tt('dve', pair(BW, sn, 128, c0), pair(BW, so, 128, c0), pair(BW, gi * 1024, 128, c0),
                               ALU.add, [s_r, sp_r], [('BW', sn, sn + 1024)])

                def wview(gi, nk):
                    if gi < 4:
                        return FW[0:nk, gi * 512:(gi + 1) * 512].bitcast(BF16), ('FW', gi * 512, gi * 512 + 512)
                    return BW[0:nk, 7168:8192], ('BW', 7168, 8192)

                def actop(a):
                    kind, hp, si, gi, r = a['kind'], a['hp'], a['si'], a['gi'], a['r']
                    kb, b0, b1, nk, diag, c0 = geom(si)
                    if kind == 'sp':
                        act(pair(BW, gi * 1024, nk, c0), pair(PSB[r], 0, nk, c0), AF.Softplus,
                            [psr(2 * r), psr(2 * r + 1)], [('BW', gi * 1024, gi * 1024 + 1024)])
                    else:
                        wv, w_r = wview(gi, nk)
                        act(wv.rearrange("p (h c) -> p h c", h=2)[:, :, c0:N], pair(PSB[r], 0, nk, c0), AF.Exp,
                            [psr(2 * r), psr(2 * r + 1)], [w_r])

                pending = []

                def post(a):
                    if a['kind'] != 'ex':
                        return
                    hp, si, gi = a['hp'], a['si'], a['gi']

                    def emit_av():
                        kb, b0, b1, nk, diag, c0 = geom(si)
                        wv, w_r = wview(gi, nk)
                        if si == 0:
                            mm(PS[PO][:, 0:N], ZER[:, 0:128], CST[:, 0:N], True, False, [('ZER', 0, 128), r_cst],
                               [psr(PO)], False)
                        last = (si == n - 1)
                        for hh in range(2):
                            h = 2 * hp + hh
                            w_ap = wv[:, hh * 512 + c0:hh * 512 + N]
                            mm(PS[PO][64 * hh:64 * hh + 64, c0:N], V.ap(kb, h * 64, h * 64 + 64, 0, nk), w_ap,
                               False, last, [V.r(kb, 0, 512), w_r], [psr(PO)], hh == 1, tp=(0, 64 * hh))
                        if last:
                            cp('dve', Q.ap(hp, t0, t1), PS[PO][:, 0:N], [psr(PO)], [Q.r(hp, t0, t1)])
                    pending.append(emit_av)

                entries = [(hp, si) for hp in range(4) for si in range(n)]
                acts = []
                for j in range(0, len(entries), GATT):
                    grp = entries[j:j + GATT]
                    for gi, (hp, si) in enumerate(grp):
                        acts.append(dict(kind='sp', hp=hp, si=si, gi=gi))
                    for gi, (hp, si) in enumerate(grp):
                        acts.append(dict(kind='ex', hp=hp, si=si, gi=gi))
                for k in range(min(2, len(acts))):
                    prep(acts[k])
                for k in range(len(acts)):
                    if k + 2 < len(acts):
                        prep(acts[k + 2])
                    if acts[k]['kind'] == 'ex' and acts[k]['gi'] == 0:
                        while pending:
                            pending.pop(0)()
                    actop(acts[k])
                    post(acts[k])
                    if acts[k]['kind'] == 'sp' and pending:
                        pending.pop(0)()
                    if work is not None and acts[k]['kind'] == 'sp' and acts[k]['gi'] % 2 == 0 and work.get(i):
                        work[i].pop(0)()
                while pending:
                    pending.pop(0)()
                if work is not None:
                    while work.get(i):
                        work[i].pop(0)()
                    if after_work is not None and i == max(work):
                        after_work()
            ps_pool[0] = list(range(8))

        def merge_slab(l, j, sgcnt, ft=0):
            us = (0, 1, 2) if j % 2 == 0 else (3, 4, 5)
            Tg, ang, offg = wunit(us[0])
            Ts, ans, offs = wunit(us[1])
            Tb, anb, offb = wunit(us[2])
            for i in range(ft, 5):
                t0, t1 = TT[i]
                N = t1 - t0
                for nn in range(2):
                    n = 2 * j + nn
                    pbs = [newps() for _ in range(4)]
                    for kc in range(8):
                        mm(PS[pbs[0]][:, 0:N], Tg[:, offg + kc * 256 + nn * 128:offg + kc * 256 + nn * 128 + 128],
                           H.ap(kc, t0, t1), kc == 0, kc == 7, [(ang, offg, offg + 2048), H.r(kc, t0, t1)],
                           [psr(pbs[0])], kc == 7)
                    for kc in range(8):
                        mm(PS[pbs[1]][:, 0:N], Ts[:, offs + kc * 256 + nn * 128:offs + kc * 256 + nn * 128 + 128],
                           H.ap(kc, t0, t1), kc == 0, kc == 7, [(ans, offs, offs + 2048), H.r(kc, t0, t1)],
                           [psr(pbs[1])], kc == 7)
                    for c in range(4):
                        o_ = offb + c * 256 + nn * 128
                        mm(PS[pbs[2]][:, 0:N], Tb[:, o_:o_ + 128], A.ap(c, t0, t1), c == 0, c == 3,
                           [(anb, offb, offb + 2048), A.r(c, t0, t1)], [psr(pbs[2])], c == 3)
                    for c in range(4):
                        o_ = offb + 1024 + c * 256 + nn * 128
                        mm(PS[pbs[3]][:, 0:N], Tb[:, o_:o_ + 128], Q.ap(c, t0, t1), c == 0, c == 3,
                           [(anb, offb, offb + 2048), Q.r(c, t0, t1)], [psr(pbs[3])], c == 3)
                    k3 = sgcnt[0] % 2
                    sgcnt[0] += 1
                    g1, rg1 = fw(k3 * 1024, k3 * 1024 + N)
                    g2, rg2 = fw(k3 * 1024 + 512, k3 * 1024 + 512 + N)
                    bc = PC_BG + l * 16
                    act(g1, PS[pbs[0]][:, 0:N], AF.Sigmoid, [psr(pbs[0]), ('PRM', 0, NPC)], [rg1],
                        bias=PRM[:, bc + n:bc + n + 1])
                    act(g2, PS[pbs[1]][:, 0:N], AF.Sigmoid, [psr(pbs[1]), ('PRM', 0, NPC)], [rg2],
                        bias=PRM[:, bc + 8 + n:bc + 8 + n + 1])
                    tt('dve', g1, g1, PS[pbs[2]][:, 0:N], ALU.mult, [rg1, psr(pbs[2])], [rg1])
                    tt('dve', g2, g2, PS[pbs[3]][:, 0:N], ALU.mult, [rg2, psr(pbs[3])], [rg2])
                    tt('dve', MG.ap(n, t0, t1), g1, g2, ALU.add, [rg1, rg2], [MG.r(n, t0, t1)])

        def out_proj(l, post, ft=0):
            for i in range(ft, 5):
                t0, t1 = TT[i]
                N = t1 - t0
                if i > ft:
                    norm_a(i - 1)
                for n in range(8):
                    jj, nn = divmod(n, 4)
                    off = jj * 4096
                    rw = ('WB', off, off + 4096)
                    pb = newps()
                    for kc in range(8):
                        mm(PS[pb][:, 0:N], WB[:, off + kc * 512 + nn * 128:off + kc * 512 + nn * 128 + 128],
                           MG.ap(kc, t0, t1), kc == 0, kc == 7, [rw, MG.r(kc, t0, t1)], [psr(pb)], kc == 7)
                    tt('dve', X.ap(n, t0, t1), X.ap(n, t0, t1), PS[pb][:, 0:N], ALU.add,
                       [X.r(n, t0, t1), psr(pb)], [X.r(n, t0, t1)])
                    if n == 3 and i > ft:
                        norm_b(i - 1)
                        norm_c(i - 1, post[0], post[1])
            norm_a(4)
            norm_b(4)
            norm_c(4, post[0], post[1])

        def ffn_slab(l, j, acnt, post=None, ft=0):
            wib, wob = FFN_SLOT[j % 2]
            nfc = ffn_nfc(j)
            rws = ffn_ranges(j)
            for i in range(ft, 5):
                t0, t1 = TT[i]
                N = t1 - t0
                par = acnt[0] % 2
                acnt[0] += 1
                ab = FFN_ACT + par * 2048
                if post is not None and i > ft:
                    norm_a(i - 1)
                for fc in range(nfc):
                    pbg, pbu = newps(), newps()
                    for gu, pb in ((0, pbg), (1, pbu)):
                        for kc in range(8):
                            o_ = wib + kc * 1024 + gu * 512 + fc * 128
                            mm(PS[pb][:, 0:N], MIX[:, o_:o_ + 128], H.ap(kc, t0, t1), kc == 0, kc == 7,
                               rws + [H.r(kc, t0, t1)], [psr(pb)], kc == 7)
                    sb_ = 1024 + (fc % 2) * 512
                    sl, rsl = fw(sb_, sb_ + N)
                    act(sl, PS[pbg][:, 0:N], AF.Silu, [psr(pbg)], [rsl])
                    tt('dve', MIX[:, ab + fc * 512:ab + fc * 512 + N], sl, PS[pbu][:, 0:N], ALU.mult,
                       [rsl, psr(pbu)], [('MIX', ab + fc * 512, ab + fc * 512 + 512)])
                if post is not None and i > ft:
                    norm_b(i - 1)
                    norm_c(i - 1, post[0], post[1])
                for n in range(8):
                    pb = newps()
                    for fc in range(nfc):
                        o_ = wob + fc * 1024 + n * 128
                        mm(PS[pb][:, 0:N], MIX[:, o_:o_ + 128], MIX[:, ab + fc * 512:ab + fc * 512 + N],
                           fc == 0, fc == nfc - 1, rws + [('MIX', ab + fc * 512, ab + fc * 512 + 512)],
                           [psr(pb)], fc == nfc - 1)
                    tt('dve', X.ap(n, t0, t1), X.ap(n, t0, t1), PS[pb][:, 0:N], ALU.add,
                       [X.r(n, t0, t1), psr(pb)], [X.r(n, t0, t1)])
            if post is not None:
                norm_a(4)
                norm_b(4)
                norm_c(4, post[0], post[1])

        def dump(name, view, nchunk, tlen):
            if name not in dbg_d:
                return
            for a in range(nchunk):
                for c0 in range(0, tlen, 512):
                    c1 = min(tlen, c0 + 512)
                    stg, rst = fw(1024, 1024 + (c1 - c0))
                    cp('dve', stg, view.ap(a, c0, c1), [view.r(a, c0, c1)], [rst])
                    dst = dbg_d[name][:, a * tlen + c0:a * tlen + c1]
                    P.dma('sp', lambda e, dst=dst, stg=stg: e.dma_start(out=dst, in_=stg), 'dbg', [rst],
                          [('dbg', 0, CELL)])

        npb = {}

        def norm_a(i):
            t0, t1 = TT[i]
            N = t1 - t0
            for kc in range(8):
                sq, rsq = bw(kc * 512, kc * 512 + N)
                act(sq, X.ap(kc, t0, t1), AF.Square, [X.r(kc, t0, t1)], [rsq])

        def norm_b(i):
            t0, t1 = TT[i]
            N = t1 - t0
            pb = newps()
            npb[i] = pb
            for kc in range(8):
                sq, rsq = bw(kc * 512, kc * 512 + N)
                mm(PS[pb][:, 0:N], NEGONES, sq, kc == 0, kc == 7, [rsq, r_cst], [psr(pb)], kc == 7)

        def norm_c(i, gcol, final=False):
            t0, t1 = TT[i]
            N = t1 - t0
            pb = npb[i]
            lnv, rln = fw(0, N)
            rstd, rrs = fw(512, 512 + N)
            act(lnv, PS[pb][:, 0:N], AF.Ln, [psr(pb)], [rln], bias=EPS, scale=-1.0 / D)
            act(rstd, lnv, AF.Exp, [rln], [rrs], scale=-0.5)
            if not final:
                for kc in range(8):
                    stt(H.ap(kc, t0, t1), X.ap(kc, t0, t1), PRM[:, gcol + kc:gcol + kc + 1], rstd,
                        ALU.mult, ALU.mult, [X.r(kc, t0, t1), rrs, ('PRM', 0, NPC)], [H.r(kc, t0, t1)])
                return
            if i == 0:
                return
            for kc in range(8):
                lo = 4096 + (kc % 4) * 1024
                stg = BW[:, lo:lo + 2 * N].bitcast(F32)
                rst = ('BW', lo, lo + 1024)
                stt(stg, X.ap(kc, t0, t1), PRM[:, PC_FN + kc:PC_FN + kc + 1], rstd, ALU.mult, ALU.mult,
                    [X.r(kc, t0, t1), rrs, ('PRM', 0, NPC)], [rst])
                dst = y_d[kc * 128:(kc + 1) * 128, t0 - 16:t1 - 16]
                P.dma('sp', lambda e, dst=dst, stg=stg: e.dma_start(out=dst, in_=stg), 'out%d' % (kc % 4), [rst],
                      [('out', 0, CELL)])

        load_pmx(0)
        load_in_slab(0, 0, 0)
        for l in range(depth):
            load_in_slab(l, 1536, 2)
            if l == 0:
                for i in range(5):
                    norm_to_h(i, PC_N1 + l * 8)
                dump('h', H, 8, NT)
            ft = 1 if l == depth - 1 else 0
            ecnt = [0]
            if ft == 0:
                proj_fm(l, 0, 0, 'q')
            proj_fm(l, 1, 0, 'q')
            for i in range(5):
                proj_u(l, i, 2, ecnt)
                if i + 2 < 5:
                    proj_fm(l, i + 2, 0, 'q')
                if i == 2:
                    load_in_slab(l, 512, 0)
                if i >= ft:
                    pool_band(l, i)
            load_in_slab(l, 1024, 2)
            for i in range(2):
                proj_fm(l, i, 0, 'k')
            for i in range(2):
                proj_v(l, i, 2)
            work = {}
            for i in range(1, 4):
                work[i] = proj_fm_items(l, i + 1, 0, 'k', eng='dve') + proj_v_items(l, i + 1, 2)
            if l == 0:
                dump('q', Q, 4, NT)
            attention(work, after_work=lambda l=l: load_merge_slab(l, 0), ft=ft)
            if l == 0:
                dump('s', Q, 4, NT)
                dump('k', K, 4, NT)
                dump('v', V, 17, 512)
                dump('a', A, 4, NT)
            sgcnt = [0]
            for j in range(4):
                if j + 1 < 4:
                    load_merge_slab(l, j + 1)
                merge_slab(l, j, sgcnt, ft)
                if j == 2:
                    load_out_slab(l, 0)
            if l == 0:
                dump('mg', MG, 8, NT)
            load_out_slab(l, 1)
            load_ffn_slab(l, 0)
            out_proj(l, (PC_N2 + l * 8, False), ft)
            if l == 0:
                dump('x1', X, 8, NT)
            acnt = [0]
            for j in range(6):
                if j + 1 < 6:
                    load_ffn_slab(l, j + 1)
                elif l + 1 < depth:
                    load_pmx(l + 1)
                    load_in_slab(l + 1, 0, 0)
                if j < 5:
                    ffn_slab(l, j, acnt, ft=ft)
                elif l + 1 < depth:
                    ffn_slab(l, j, acnt, post=(PC_N1 + (l + 1) * 8, False))
                else:
                    ffn_slab(l, j, acnt, post=(PC_FN, True), ft=ft)
            if l == 0:
                dump('x2', X, 8, NT)
        P.op('sp', None, reads=[('out', 0, CELL), ('dbg', 0, CELL)])
        P.emit()
    return nc


def host_consts():
    c = np.zeros((128, NCST), np.float32)
    j = np.arange(128)[:, None]
    s = np.arange(128)[None, :]
    c[:, 0:128] = np.eye(128, dtype=np.float32)
    c[:, 128:256] = -1.0 * (j >= s)
    c[:, 256:384] = np.where(j >= s, NEG, 0.0)
    t = np.arange(128)[:, None]
    tp = np.arange(128)[None, :]
    for g in range(4):
        w = 2 << g
        gb = 384 + g * 400
        d = tp - t
        c[:, gb:gb + 128] = np.where((d >= 0) & (d <= w - 1), 1.0 / w, 0.0) - (d == 0)
        c[:, gb + 128:gb + 256] = np.where((128 + tp) - t <= w - 1, 1.0 / w, 0.0)
        t16 = np.arange(16)[:, None]
        tp16 = np.arange(16)[None, :]
        d16 = tp16 - t16
        cnt = np.minimum(tp16 + 1, w).astype(np.float32)
        c[0:16, gb + 256:gb + 272] = np.where((d16 >= 0) & (d16 <= w - 1), 1.0 / cnt, 0.0) - (d16 == 0)
        c[0:16, gb + 272:gb + 400] = np.where((16 + tp) - t16 <= w - 1, 1.0 / w, 0.0)
    return c


def host_params(norm1_g, norm2_g, final_norm_g, b_gate, pool_scale):
    p = np.zeros((128, NPC), np.float32)
    for l in range(DEPTH):
        p[:, PC_N1 + l * 8:PC_N1 + l * 8 + 8] = np.asarray(norm1_g[l]).reshape(8, 128).T
        p[:, PC_N2 + l * 8:PC_N2 + l * 8 + 8] = np.asarray(norm2_g[l]).reshape(8, 128).T
        p[:, PC_BG + l * 16:PC_BG + l * 16 + 16] = np.asarray(b_gate[l]).reshape(16, 128).T
        p[:, PC_PS + l * 4:PC_PS + l * 4 + 4] = np.asarray(pool_scale[l]).reshape(4, 128).T
    p[:, PC_FN:PC_FN + 8] = np.asarray(final_norm_g).reshape(8, 128).T
    return p


def make_in_maps(x, meta_tokens, norm1_g, w_in, b_gate, pool_mix, pool_scale, w_branch_pool,
                 w_branch_sb, w_out, norm2_g, w_ffn_in, w_ffn_out, final_norm_g):
    f = lambda a: np.ascontiguousarray(np.asarray(a, dtype=np.float32))
    x = f(x)
    shared = dict(
        metaT=f(np.asarray(meta_tokens, np.float32).T),
        params=host_params(norm1_g, norm2_g, final_norm_g, b_gate, pool_scale),
        consts=host_consts(),
        w_in=f(w_in), pool_mix=f(pool_mix), w_branch_pool=f(w_branch_pool), w_branch_sb=f(w_branch_sb),
        w_out=f(w_out), w_ffn_in=f(w_ffn_in), w_ffn_out=f(w_ffn_out),
    )
    maps = []
    for b in range(x.shape[0]):
        m = dict(shared)
        m["xT"] = f(x[b].T)
        maps.append(m)
    return maps


_NC_CACHE = {}


def kernel(x, meta_tokens, norm1_g, w_in, b_gate, pool_mix, pool_scale, w_branch_pool,
           w_branch_sb, w_out, norm2_g, w_ffn_in, w_ffn_out, final_norm_g):
    in_maps = make_in_maps(x, meta_tokens, norm1_g, w_in, b_gate, pool_mix, pool_scale, w_branch_pool,
                           w_branch_sb, w_out, norm2_g, w_ffn_in, w_ffn_out, final_norm_g)
    nc = build()
    res = run_bass_kernel_spmd(nc, in_maps, core_ids=list(range(8)))
    out = np.stack([np.ascontiguousarray(r["yT"].T) for r in res.results], axis=0)
    return out.astype(np.float32)
```

```python
import contextlib
import numpy as np
import concourse.bass as bass
import concourse.mybir as mybir
from concourse.bass_utils import run_bass_kernel_spmd

F32 = mybir.dt.float32
BF16 = mybir.dt.bfloat16
AF = mybir.ActivationFunctionType
ALU = mybir.AluOpType

D = 1024
SEQ = 2048
NMETA = 16
NT = SEQ + NMETA
DFF = 2816
DEPTH = 2
EPS = 1e-6
TT = [(0, 16)] + [(16 + 512 * i, 16 + 512 * (i + 1)) for i in range(4)]
KB = [(0, 16)] + [(16 + 128 * j, 144 + 128 * j) for j in range(16)]
CELL = 16
NEG = -30000.0
GATT = 5
FLAGS = dict(av_tp00=True, mask_after=True, pair_act=True)

PC_N1 = 0
PC_N2 = 16
PC_FN = 32
PC_BG = 40
PC_PS = 72
NPC = 80
NBAND = 1600
NCST = 384 + NBAND


class Prog:
    def __init__(self, nc):
        self.nc = nc
        self.ops = []
        self.cnt = {}
        self.arenas = {}

    def arena(self, name, nelem):
        self.arenas[name] = dict(n=(nelem + CELL - 1) // CELL, W={}, R={})

    def _record(self, eng, fn, reads, writes, stream, val, skip):
        need = {}
        for (an, lo, hi) in reads:
            A = self.arenas[an]
            c0, c1 = lo // CELL, (hi + CELL - 1) // CELL
            for s, arr in A['W'].items():
                v = int(arr[c0:c1].max())
                if v > need.get(s, 0):
                    need[s] = v
        for (an, lo, hi) in writes:
            A = self.arenas[an]
            c0, c1 = lo // CELL, (hi + CELL - 1) // CELL
            for tab in (A['W'], A['R']):
                for s, arr in tab.items():
                    v = int(arr[c0:c1].max())
                    if v > need.get(s, 0):
                        need[s] = v
        for s in skip:
            need.pop(s, None)
        for s, v in need.items():
            if s.startswith('E:'):
                assert v <= self.cnt.get(s, 0), ("dependency on a not-yet-issued inc", s, v)
        for (an, lo, hi) in reads:
            A = self.arenas[an]
            c0, c1 = lo // CELL, (hi + CELL - 1) // CELL
            if stream not in A['R']:
                A['R'][stream] = np.zeros(A['n'], np.int64)
            A['R'][stream][c0:c1] = val
        for (an, lo, hi) in writes:
            A = self.arenas[an]
            c0, c1 = lo // CELL, (hi + CELL - 1) // CELL
            if stream not in A['W']:
                A['W'][stream] = np.zeros(A['n'], np.int64)
            A['W'][stream][c0:c1] = val
        return need

    def op(self, eng, fn, reads=(), writes=(), inc=True):
        stream = 'E:' + eng
        val = self.cnt.get(stream, 0) + 1
        skip = (stream,) if eng == 'pe' else ()
        need = self._record(eng, fn, reads, writes, stream, val, skip)
        if inc:
            self.cnt[stream] = val
        self.ops.append(dict(eng=eng, fn=fn, need=need, inc=inc, dma=None))

    def dma(self, queue, fn, sem, reads=(), writes=()):
        stream = 'D:' + sem
        val = self.cnt.get(stream, 0) + 16
        need = self._record(queue, fn, reads, writes, stream, val, (stream,))
        self.cnt[stream] = val
        self.ops.append(dict(eng=queue, fn=fn, need=need, inc=True, dma=sem))

    def emit(self):
        nc = self.nc
        ops = self.ops
        sem_names = [s for s in self.cnt if self.cnt[s] > 0]
        with contextlib.ExitStack() as st:
            sems = {}
            for n in sem_names:
                sems[n] = st.enter_context(nc.semaphore(n.replace(':', '_')))
            block = st.enter_context(nc.Block())

            def run(ename, eng):
                waited = {}
                for o in ops:
                    if o['eng'] != ename:
                        continue
                    for s, v in o['need'].items():
                        if waited.get(s, 0) < v:
                            eng.wait_ge(sems[s], v)
                            waited[s] = v
                    if o['fn'] is None:
                        continue
                    ins = o['fn'](eng)
                    if o['dma'] is not None:
                        ins.then_inc(sems['D:' + o['dma']], 16)
                    elif o['inc']:
                        ins.then_inc(sems['E:' + ename], 1)

            used = set(o['eng'] for o in ops)
            if 'pe' in used:
                @block.tensor
                def _(e):
                    run('pe', e)
            if 'act' in used:
                @block.scalar
                def _(e):
                    run('act', e)
            if 'dve' in used:
                @block.vector
                def _(e):
                    run('dve', e)
            if 'pool' in used:
                @block.gpsimd
                def _(e):
                    run('pool', e)
            if 'sp' in used:
                @block.sync
                def _(e):
                    run('sp', e)


class View:
    def __init__(self, tens, name, off, A, T):
        self.tens, self.name, self.off, self.A, self.T = tens, name, off, A, T

    def ap(self, a, t0, t1, p0=0, p1=128):
        b = self.off + a * self.T
        return self.tens[p0:p1, b + t0:b + t1]

    def r(self, a, t0, t1):
        b = self.off + a * self.T
        return (self.name, b + t0, b + t1)


def build(depth=DEPTH, dbg=None):
    nc = bass.Bass("TRN2", target_bir_lowering=False)
    dt = nc.dram_tensor
    xT_d = dt("xT", [D, SEQ], F32, kind="ExternalInput").ap()
    meta_d = dt("metaT", [D, NMETA], F32, kind="ExternalInput").ap()
    prm_d = dt("params", [128, NPC], F32, kind="ExternalInput").ap()
    cst_d = dt("consts", [128, NCST], F32, kind="ExternalInput").ap()
    w_in_d = dt("w_in", [DEPTH, D, 4096], F32, kind="ExternalInput").ap()
    pmix_d = dt("pool_mix", [DEPTH, 4, 128, 128], F32, kind="ExternalInput").ap()
    wbp_d = dt("w_branch_pool", [DEPTH, 512, D], F32, kind="ExternalInput").ap()
    wbs_d = dt("w_branch_sb", [DEPTH, 512, D], F32, kind="ExternalInput").ap()
    wout_d = dt("w_out", [DEPTH, D, D], F32, kind="ExternalInput").ap()
    wfi_d = dt("w_ffn_in", [DEPTH, D, 2 * DFF], F32, kind="ExternalInput").ap()
    wfo_d = dt("w_ffn_out", [DEPTH, DFF, D], F32, kind="ExternalInput").ap()
    y_d = dt("yT", [D, SEQ], F32, kind="ExternalOutput").ap()
    dbg_d = {}
    if dbg:
        for name, shape in dbg.items():
            dbg_d[name] = dt("dbg_" + name, shape, F32, kind="ExternalOutput").ap()

    P = Prog(nc)
    with contextlib.ExitStack() as st:
        def sb(name, n, dtype):
            P.arena(name, n)
            return st.enter_context(nc.sbuf_tensor(name, [128, n], dtype))

        Xt = sb("X", 8 * NT, F32)
        PRM = sb("PRM", NPC, F32)
        FW = sb("FW", 2048, F32)
        Ht = sb("H", 8 * NT, BF16)
        MIX = sb("MIX", 33472, BF16)
        WB = sb("WB", 4 * 2048, BF16)
        CST = sb("CST", 512 + NBAND, BF16)
        BW = sb("BW", 8192, BF16)
        ZER = sb("ZER", 128, BF16)
        PMX = sb("PMX", 512, BF16)
        PSB = [st.enter_context(nc.psum_tensor("psb%d" % b, [128, 1024], F32)) for b in range(4)]

        class _Bank:
            def __init__(self, b):
                self.t, self.o = PSB[b // 2], (b % 2) * 512

            def __getitem__(self, idx):
                ps_, cs_ = idx
                c0 = cs_.start or 0
                c1 = 512 if cs_.stop is None else cs_.stop
                return self.t[ps_, self.o + c0:self.o + c1]

        PS = [_Bank(b) for b in range(8)]
        P.arena('ps', 8 * CELL)
        P.arena('out', CELL)
        P.arena('dbg', CELL)

        X = View(Xt, "X", 0, 8, NT)
        H = View(Ht, "H", 0, 8, NT)
        Q = View(MIX, "MIX", 0, 4, NT)
        K = View(MIX, "MIX", 8256, 4, NT)
        V = View(MIX, "MIX", 16512, 17, 512)
        A = View(MIX, "MIX", 25216, 4, NT)
        MG = View(MIX, "MIX", 8256, 8, NT)

        def psr(b):
            return ('ps', b * CELL, (b + 1) * CELL)

        def fw(lo, hi):
            return FW[:, lo:hi], ('FW', lo, hi)

        def bw(lo, hi):
            return BW[:, lo:hi], ('BW', lo, hi)

        IDENT = CST[:, 0:128]
        NEGU = CST[:, 128:256]
        NEGM = CST[:, 256:384]
        NEGONES = CST[:, 384:512]
        r_cst = ('CST', 0, 512)

        psn = [0]
        ps_pool = [list(range(8))]

        def newps():
            pool_ = ps_pool[0]
            b = pool_[psn[0] % len(pool_)]
            psn[0] += 1
            return b

        def mm(out, lhsT, rhs, start, stop, reads, writes, inc, tp=None):
            if tp is None:
                P.op('pe', lambda e: e.matmul(out, lhsT=lhsT, rhs=rhs, start=start, stop=stop),
                     reads, writes, inc)
            else:
                P.op('pe', lambda e: e.matmul(out, lhsT=lhsT, rhs=rhs, start=start, stop=stop,
                                              tile_position=tp), reads, writes, inc)

        def mmx(out, lhsT, rhs, start, stop, reads, writes, inc):
            P.op('pe', lambda e: e.matmul(out, lhsT=lhsT, rhs=rhs, start=start, stop=stop,
                                          skip_group_check=True), reads, writes, inc)

        def act(out, in_, func, reads, writes, bias=None, scale=None):
            kw = {}
            if bias is not None:
                kw['bias'] = bias
            if scale is not None:
                kw['scale'] = scale
            P.op('act', lambda e: e.activation(out=out, in_=in_, func=func, **kw), reads, writes)

        def tt(eng, out, in0, in1, op, reads, writes):
            P.op(eng, lambda e: e.tensor_tensor(out=out, in0=in0, in1=in1, op=op), reads, writes)

        def stt(out, in0, scalar, in1, op0, op1, reads, writes):
            P.op('dve', lambda e: e.scalar_tensor_tensor(out=out, in0=in0, scalar=scalar, in1=in1,
                                                         op0=op0, op1=op1), reads, writes)

        def cp(eng, out, in_, reads, writes):
            if eng == 'act':
                P.op(eng, lambda e: e.copy(out=out, in_=in_), reads, writes)
            else:
                P.op(eng, lambda e: e.tensor_copy(out=out, in_=in_), reads, writes)

        def memset(eng, ap, val, writes):
            P.op(eng, lambda e: e.memset(ap, val), (), writes)

        def wdma(out, in_, sem, writes):
            P.dma('pool', lambda e: e.dma_start(out=out, in_=in_), sem, (), writes)

        P.dma('sp', lambda e: e.dma_start(out=PRM[:, :], in_=prm_d), 'prm', (), [('PRM', 0, NPC)])
        wdma(CST[:, 0:384], cst_d[:, 0:384], 'cst', [('CST', 0, 384)])
        wdma(CST[:, 512:512 + NBAND], cst_d[:, 384:NCST], 'cst2', [('CST', 512, 512 + NBAND)])
        memset('dve', CST[:, 384:512], -1.0, [('CST', 384, 512)])
        memset('dve', ZER[:, :], 0.0, [('ZER', 0, 128)])
        X3 = Xt[:, :].rearrange("p (k t) -> p k t", t=NT)
        P.dma('sp', lambda e: e.dma_start(out=X3[:, :, 0:16], in_=meta_d.rearrange("(k p) t -> p k t", p=128)),
              'x0', (), [X.r(kc, 0, 16) for kc in range(8)])
        for i in range(1, 5):
            t0, t1 = TT[i]
            src = xT_d[:, t0 - 16:t1 - 16].rearrange("(k p) t -> p k t", p=128)
            dst = X3[:, :, t0:t1]
            P.dma('sp', lambda e, dst=dst, src=src: e.dma_start(out=dst, in_=src),
                  'x%d' % i, (), [X.r(kc, t0, t1) for kc in range(8)])

        def wunit(u):
            if u < 4:
                return WB, 'WB', u * 2048
            return BW, 'BW', (u - 4) * 2048

        def load_in_slab(l, col0, u0):
            off = u0 * 2048
            dst = WB[:, off:off + 4096].rearrange("p (k n) -> p k n", n=512)
            src = w_in_d[l, :, col0:col0 + 512].rearrange("(k p) n -> p k n", p=128)
            wdma(dst, src, 'wu%d' % u0, [('WB', off, off + 4096)])

        def load_merge_slab(l, j):
            us = (0, 1, 2) if j % 2 == 0 else (3, 4, 5)
            for idx, col0 in ((0, 2048 + 256 * j), (1, 3072 + 256 * j)):
                T_, an, off = wunit(us[idx])
                dst = T_[:, off:off + 2048].rearrange("p (k n) -> p k n", n=256)
                src = w_in_d[l, :, col0:col0 + 256].rearrange("(k p) n -> p k n", p=128)
                wdma(dst, src, 'wu%d' % us[idx], [(an, off, off + 2048)])
            T_, an, off = wunit(us[2])
            for idx, wd in ((0, wbp_d), (1, wbs_d)):
                dst = T_[:, off + idx * 1024:off + idx * 1024 + 1024].rearrange("p (k n) -> p k n", n=256)
                src = wd[l, :, 256 * j:256 * j + 256].rearrange("(k p) n -> p k n", p=128)
                wdma(dst, src, 'wu%d' % us[2], [(an, off, off + 2048)])

        def load_out_slab(l, jj):
            off = jj * 4096
            dst = WB[:, off:off + 4096].rearrange("p (k n) -> p k n", n=512)
            src = wout_d[l, :, 512 * jj:512 * jj + 512].rearrange("(k p) n -> p k n", p=128)
            wdma(dst, src, 'wu%d' % (2 * jj), [('WB', off, off + 4096)])

        FFN_SLABS = [(0, 2), (2, 4), (6, 4), (10, 4), (14, 4), (18, 4)]

        def ffn_nfc(j):
            return FFN_SLABS[j][1]

        FFN_SLOT = [(0, 25216), (8256, 16448)]
        FFN_ACT = 29312

        def ffn_ranges(j):
            wib, wob = FFN_SLOT[j % 2]
            return [('MIX', wib, wib + 8192), ('MIX', wob, wob + 4096)]

        def load_ffn_slab(l, j):
            wib, wob = FFN_SLOT[j % 2]
            nfc = ffn_nfc(j)
            nc_ = 128 * nfc
            wi4 = MIX[:, wib:wib + 8192].rearrange("p (k g c) -> p k g c", k=8, g=2)
            sem = 'f%d' % (j % 2)
            for gu in range(2):
                fo = FFN_SLABS[j][0] * 128
                src = wfi_d[l, :, gu * DFF + fo:gu * DFF + fo + nc_].rearrange("(k p) n -> p k n", p=128)
                wdma(wi4[:, :, gu, 0:nc_], src, sem, ffn_ranges(j))
            dst = MIX[:, wob:wob + nfc * 1024].rearrange("p (f n) -> p f n", n=1024)
            fo = FFN_SLABS[j][0] * 128
            src = wfo_d[l, fo:fo + nc_, :].rearrange("(f p) n -> p f n", p=128)
            wdma(dst, src, sem, ffn_ranges(j))

        def load_pmx(l):
            dst = PMX[:, :].rearrange("p (g d) -> p g d", d=128)
            src = pmix_d[l].rearrange("g c d -> c g d")
            wdma(dst, src, 'pmx', [('PMX', 0, 512)])

        def norm_stats(i):
            t0, t1 = TT[i]
            N = t1 - t0
            pb = newps()
            for kc in range(8):
                sq, rsq = bw((kc % 2) * 512, (kc % 2) * 512 + N)
                act(sq, X.ap(kc, t0, t1), AF.Square, [X.r(kc, t0, t1)], [rsq])
                mm(PS[pb][:, 0:N], NEGONES, sq, kc == 0, kc == 7, [rsq, r_cst], [psr(pb)], True)
            lnv, rln = fw(0, N)
            rstd, rrs = fw(512, 512 + N)
            act(lnv, PS[pb][:, 0:N], AF.Ln, [psr(pb)], [rln], bias=EPS, scale=-1.0 / D)
            act(rstd, lnv, AF.Exp, [rln], [rrs], scale=-0.5)
            return rstd, rrs

        def norm_to_h(i, gcol):
            t0, t1 = TT[i]
            rstd, rrs = norm_stats(i)
            for kc in range(8):
                stt(H.ap(kc, t0, t1), X.ap(kc, t0, t1), PRM[:, gcol + kc:gcol + kc + 1], rstd,
                    ALU.mult, ALU.mult, [X.r(kc, t0, t1), rrs, ('PRM', 0, NPC)], [H.r(kc, t0, t1)])

        def tile_kbs(i):
            return [0] if i == 0 else list(range(4 * (i - 1) + 1, 4 * i + 1))

        r_band = ('CST', 512, 512 + NBAND)

        def proj_u(l, i, u0, ecnt):
            off = u0 * 2048
            rw = ('WB', off, off + 4096)
            for kb in tile_kbs(i):
                b0, b1 = KB[kb]
                nk = b1 - b0
                pb = newps()
                for kc in range(8):
                    mm(PS[pb][0:nk, 0:512], H.ap(kc, b0, b1), WB[:, off + kc * 512:off + kc * 512 + 512],
                       kc == 0, kc == 7, [rw, H.r(kc, b0, b1)], [psr(pb)], kc == 7)
                ub = 2048 + (kb % 6) * 512
                eng = 'act' if ecnt[0] % 2 == 0 else 'dve'
                ecnt[0] += 1
                cp(eng, BW[0:nk, ub:ub + 512], PS[pb][0:nk, 0:512], [psr(pb)], [('BW', ub, ub + 512)])

        def pool_band(l, i):
            t0, t1 = TT[i]
            N = t1 - t0
            kbs = tile_kbs(i)
            pend = []

            def band(g):
                pd = newps()
                gb = 512 + g * 400
                for j, kb in enumerate(kbs):
                    b0, b1 = KB[kb]
                    nk = b1 - b0
                    ub = 2048 + (kb % 6) * 512
                    c0 = j * 128
                    if kb == 0:
                        mmx(PS[pd][:, c0:c0 + nk], BW[0:16, ub + g * 128:ub + g * 128 + 128],
                            CST[0:16, gb + 256:gb + 272], True, True, [('BW', ub, ub + 512), r_band], [psr(pd)],
                            j == len(kbs) - 1)
                        continue
                    mmx(PS[pd][:, c0:c0 + nk], BW[:, ub + g * 128:ub + g * 128 + 128], CST[:, gb:gb + 128],
                        True, False, [('BW', ub, ub + 512), r_band], [psr(pd)], False)
                    pk = kb - 1
                    pub = 2048 + (pk % 6) * 512
                    if pk == 0:
                        mmx(PS[pd][:, c0:c0 + nk], BW[0:16, pub + g * 128:pub + g * 128 + 128],
                            CST[0:16, gb + 272:gb + 400], False, True, [('BW', pub, pub + 512), r_band],
                            [psr(pd)], j == len(kbs) - 1)
                    else:
                        mmx(PS[pd][:, c0:c0 + nk], BW[:, pub + g * 128:pub + g * 128 + 128],
                            CST[:, gb + 128:gb + 256], False, True, [('BW', pub, pub + 512), r_band],
                            [psr(pd)], j == len(kbs) - 1)
                db = 1024 + (g % 2) * 512
                cp('dve', BW[:, db:db + N], PS[pd][:, 0:N], [psr(pd)], [('BW', db, db + 512)])
                return db

            def mix(g, db):
                pb2 = newps()
                mm(PS[pb2][:, 0:N], PMX[:, g * 128:(g + 1) * 128], BW[:, db:db + N], True, True,
                   [('BW', db, db + 512), ('PMX', 0, 512)], [psr(pb2)], True)
                pc = PC_PS + l * 4 + g
                act(A.ap(g, t0, t1), PS[pb2][:, 0:N], AF.Identity, [psr(pb2), ('PRM', 0, NPC)], [A.r(g, t0, t1)],
                    scale=PRM[:, pc:pc + 1])

            d0 = band(0)
            d1 = band(1)
            mix(0, d0)
            d2 = band(2)
            mix(1, d1)
            d3 = band(3)
            mix(2, d2)
            mix(3, d3)

        def proj_fm_items(l, i, u0, kind, eng='act'):
            t0, t1 = TT[i]
            N = t1 - t0
            off = u0 * 2048
            rw = ('WB', off, off + 4096)
            items = []
            for n in range(4):
                def item(n=n):
                    pb = newps()
                    for kc in range(8):
                        mm(PS[pb][:, 0:N], WB[:, off + kc * 512 + n * 128:off + kc * 512 + n * 128 + 128],
                           H.ap(kc, t0, t1), kc == 0, kc == 7, [rw, H.r(kc, t0, t1)], [psr(pb)], kc == 7)
                    if kind == 'k':
                        cp(eng, K.ap(n, t0, t1), PS[pb][:, 0:N], [psr(pb)], [K.r(n, t0, t1)])
                    else:
                        act(Q.ap(n, t0, t1), PS[pb][:, 0:N], AF.Identity, [psr(pb)], [Q.r(n, t0, t1)], scale=0.125)
                items.append(item)
            return items

        def proj_fm(l, i, u0, kind):
            for it in proj_fm_items(l, i, u0, kind):
                it()

        def proj_v_items(l, i, u0):
            off = u0 * 2048
            rw = ('WB', off, off + 4096)
            items = []
            for kb in tile_kbs(i):
                def item(kb=kb):
                    b0, b1 = KB[kb]
                    nk = b1 - b0
                    pb = newps()
                    for kc in range(8):
                        mm(PS[pb][0:nk, 0:512], H.ap(kc, b0, b1), WB[:, off + kc * 512:off + kc * 512 + 512],
                           kc == 0, kc == 7, [rw, H.r(kc, b0, b1)], [psr(pb)], kc == 7)
                    cp('dve', V.ap(kb, 0, 512, 0, nk), PS[pb][0:nk, 0:512], [psr(pb)], [V.r(kb, 0, 512)])
                items.append(item)
            return items

        def proj_v(l, i, u0):
            for it in proj_v_items(l, i, u0):
                it()

        def attention(work=None, after_work=None, ft=0):
            PO = 6
            ps_pool[0] = [7]
            SB_S = GATT * 1024
            assert GATT <= 5
            zc = [0]
            wc = [0]
            for i in range(ft, 5):
                t0, t1 = TT[i]
                N = t1 - t0
                kbs = [0] if i == 0 else list(range(4 * i, -1, -1))
                first_diag = 0 if i == 0 else 4 * (i - 1) + 1
                n = len(kbs)

                def geom(si):
                    kb = kbs[si]
                    b0, b1 = KB[kb]
                    diag = kb >= first_diag
                    c0 = (kb - first_diag) * 128 if (diag and i > 0) else 0
                    return kb, b0, b1, b1 - b0, diag, c0

                def pair(T_, base, nk, c0):
                    return T_[0:nk, base:base + 1024].rearrange("p (h c) -> p h c", h=2)[:, :, c0:N]

                def qk(hp, si, r):
                    kb, b0, b1, nk, diag, c0 = geom(si)
                    dw = min(128, N - c0)
                    for hh in range(2):
                        pz = 2 * r + hh
                        p0, p1 = 64 * hh, 64 * hh + 64
                        mmx(PS[pz][0:nk, c0:N], K.ap(hp, b0, b1, p0, p1), Q.ap(hp, t0 + c0, t1, p0, p1),
                            True, not diag, [K.r(hp, b0, b1), Q.r(hp, t0 + c0, t1)], [psr(pz)], not diag)
                    if diag:
                        for hh in range(2):
                            pz = 2 * r + hh
                            mmx(PS[pz][0:nk, c0:c0 + dw], CST[0:nk, 0:nk], CST[0:nk, 256:256 + dw],
                                False, True, [r_cst], [psr(pz)], True)

                def prep(a):
                    kind, hp, si, gi = a['kind'], a['hp'], a['si'], a['gi']
                    kb, b0, b1, nk, diag, c0 = geom(si)
                    r = zc[0] % 3
                    zc[0] += 1
                    a['r'] = r
                    qk(hp, si, r)
                    if kind == 'ex':
                        sp_r = ('BW', gi * 1024, gi * 1024 + 1024)
                        so = SB_S + (si % 2) * 1024
                        sn = SB_S + ((si + 1) % 2) * 1024
                        s_r = ('BW', so, so + 1024)
                        if si == 0 and n > 1:
                            memset('dve', BW[:, SB_S:SB_S + 2048], 0.0, [('BW', SB_S, SB_S + 2048)])
                        for hh in range(2):
                            pz = 2 * r + hh
                            sp_ap = BW[0:nk, gi * 1024 + hh * 512 + c0:gi * 1024 + hh * 512 + N]
                            mmx(PS[pz][0:nk, c0:N], CST[0:nk, 128:128 + nk], sp_ap, False, si == 0,
                                [r_cst, sp_r], [psr(pz)], si == 0)
                            if si > 0:
                                mmx(PS[pz][0:nk, c0:N], CST[:, 384:384 + nk],
                                    BW[:, so + hh * 512 + c0:so + hh * 512 + N], False, True,
                                    [r_cst, s_r], [psr(pz)], True)
                        if si < n - 1:
                            tt('dve', pair(BW, sn, 128, c0), pair(BW, so, 128, c0), pair(BW, gi * 1024, 128, c0),
                               ALU.add, [s_r, sp_r], [('BW', sn, sn + 1024)])

                def wview(gi, nk):
                    if gi < 4:
                        return FW[0:nk, gi * 512:(gi + 1) * 512].bitcast(BF16), ('FW', gi * 512, gi * 512 + 512)
                    return BW[0:nk, 7168:8192], ('BW', 7168, 8192)

                def actop(a):
                    kind, hp, si, gi, r = a['kind'], a['hp'], a['si'], a['gi'], a['r']
                    kb, b0, b1, nk, diag, c0 = geom(si)
                    if kind == 'sp':
                        act(pair(BW, gi * 1024, nk, c0), pair(PSB[r], 0, nk, c0), AF.Softplus,
                            [psr(2 * r), psr(2 * r + 1)], [('BW', gi * 1024, gi * 1024 + 1024)])
                    else:
                        wv, w_r = wview(gi, nk)
                        act(wv.rearrange("p (h c) -> p h c", h=2)[:, :, c0:N], pair(PSB[r], 0, nk, c0), AF.Exp,
                            [psr(2 * r), psr(2 * r + 1)], [w_r])

                pending = []

                def post(a):
                    if a['kind'] != 'ex':
                        return
                    hp, si, gi = a['hp'], a['si'], a['gi']

                    def emit_av():
                        kb, b0, b1, nk, diag, c0 = geom(si)
                        wv, w_r = wview(gi, nk)
                        if si == 0:
                            mm(PS[PO][:, 0:N], ZER[:, 0:128], CST[:, 0:N], True, False, [('ZER', 0, 128), r_cst],
                               [psr(PO)], False)
                        last = (si == n - 1)
                        for hh in range(2):
                            h = 2 * hp + hh
                            w_ap = wv[:, hh * 512 + c0:hh * 512 + N]
                            mm(PS[PO][64 * hh:64 * hh + 64, c0:N], V.ap(kb, h * 64, h * 64 + 64, 0, nk), w_ap,
                               False, last, [V.r(kb, 0, 512), w_r], [psr(PO)], hh == 1, tp=(0, 64 * hh))
                        if last:
                            cp('dve', Q.ap(hp, t0, t1), PS[PO][:, 0:N], [psr(PO)], [Q.r(hp, t0, t1)])
                    pending.append(emit_av)

                entries = [(hp, si) for hp in range(4) for si in range(n)]
                acts = []
                for j in range(0, len(entries), GATT):
                    grp = entries[j:j + GATT]
                    for gi, (hp, si) in enumerate(grp):
                        acts.append(dict(kind='sp', hp=hp, si=si, gi=gi))
                    for gi, (hp, si) in enumerate(grp):
                        acts.append(dict(kind='ex', hp=hp, si=si, gi=gi))
                for k in range(min(2, len(acts))):
                    prep(acts[k])
                for k in range(len(acts)):
                    if k + 2 < len(acts):
                        prep(acts[k + 2])
                    if acts[k]['kind'] == 'ex' and acts[k]['gi'] == 0:
                        while pending:
                            pending.pop(0)()
                    actop(acts[k])
                    post(acts[k])
                    if acts[k]['kind'] == 'sp' and pending:
                        pending.pop(0)()
                    if work is not None and acts[k]['kind'] == 'sp' and acts[k]['gi'] % 2 == 0 and work.get(i):
                        work[i].pop(0)()
                while pending:
                    pending.pop(0)()
                if work is not None:
                    while work.get(i):
                        work[i].pop(0)()
                    if after_work is not None and i == max(work):
                        after_work()
            ps_pool[0] = list(range(8))

        def merge_slab(l, j, sgcnt, ft=0):
            us = (0, 1, 2) if j % 2 == 0 else (3, 4, 5)
            Tg, ang, offg = wunit(us[0])
            Ts, ans, offs = wunit(us[1])
            Tb, anb, offb = wunit(us[2])
            for i in range(ft, 5):
                t0, t1 = TT[i]
                N = t1 - t0
                for nn in range(2):
                    n = 2 * j + nn
                    pbs = [newps() for _ in range(4)]
                    for kc in range(8):
                        mm(PS[pbs[0]][:, 0:N], Tg[:, offg + kc * 256 + nn * 128:offg + kc * 256 + nn * 128 + 128],
                           H.ap(kc, t0, t1), kc == 0, kc == 7, [(ang, offg, offg + 2048), H.r(kc, t0, t1)],
                           [psr(pbs[0])], kc == 7)
                    for kc in range(8):
                        mm(PS[pbs[1]][:, 0:N], Ts[:, offs + kc * 256 + nn * 128:offs + kc * 256 + nn * 128 + 128],
                           H.ap(kc, t0, t1), kc == 0, kc == 7, [(ans, offs, offs + 2048), H.r(kc, t0, t1)],
                           [psr(pbs[1])], kc == 7)
                    for c in range(4):
                        o_ = offb + c * 256 + nn * 128
                        mm(PS[pbs[2]][:, 0:N], Tb[:, o_:o_ + 128], A.ap(c, t0, t1), c == 0, c == 3,
                           [(anb, offb, offb + 2048), A.r(c, t0, t1)], [psr(pbs[2])], c == 3)
                    for c in range(4):
                        o_ = offb + 1024 + c * 256 + nn * 128
                        mm(PS[pbs[3]][:, 0:N], Tb[:, o_:o_ + 128], Q.ap(c, t0, t1), c == 0, c == 3,
                           [(anb, offb, offb + 2048), Q.r(c, t0, t1)], [psr(pbs[3])], c == 3)
                    k3 = sgcnt[0] % 2
                    sgcnt[0] += 1
                    g1, rg1 = fw(k3 * 1024, k3 * 1024 + N)
                    g2, rg2 = fw(k3 * 1024 + 512, k3 * 1024 + 512 + N)
                    bc = PC_BG + l * 16
                    act(g1, PS[pbs[0]][:, 0:N], AF.Sigmoid, [psr(pbs[0]), ('PRM', 0, NPC)], [rg1],
                        bias=PRM[:, bc + n:bc + n + 1])
                    act(g2, PS[pbs[1]][:, 0:N], AF.Sigmoid, [psr(pbs[1]), ('PRM', 0, NPC)], [rg2],
                        bias=PRM[:, bc + 8 + n:bc + 8 + n + 1])
                    tt('dve', g1, g1, PS[pbs[2]][:, 0:N], ALU.mult, [rg1, psr(pbs[2])], [rg1])
                    tt('dve', g2, g2, PS[pbs[3]][:, 0:N], ALU.mult, [rg2, psr(pbs[3])], [rg2])
                    tt('dve', MG.ap(n, t0, t1), g1, g2, ALU.add, [rg1, rg2], [MG.r(n, t0, t1)])

        def out_proj(l, post, ft=0):
            for i in range(ft, 5):
                t0, t1 = TT[i]
                N = t1 - t0
                if i > ft:
                    norm_a(i - 1)
                for n in range(8):
                    jj, nn = divmod(n, 4)
                    off = jj * 4096
                    rw = ('WB', off, off + 4096)
                    pb = newps()
                    for kc in range(8):
                        mm(PS[pb][:, 0:N], WB[:, off + kc * 512 + nn * 128:off + kc * 512 + nn * 128 + 128],
                           MG.ap(kc, t0, t1), kc == 0, kc == 7, [rw, MG.r(kc, t0, t1)], [psr(pb)], kc == 7)
                    tt('dve', X.ap(n, t0, t1), X.ap(n, t0, t1), PS[pb][:, 0:N], ALU.add,
                       [X.r(n, t0, t1), psr(pb)], [X.r(n, t0, t1)])
                    if n == 3 and i > ft:
                        norm_b(i - 1)
                        norm_c1(i - 1)
                if i > ft:
                    norm_c2(i - 1, post[0], post[1])
            norm_a(4)
            norm_b(4)
            norm_c1(4)
            norm_c2(4, post[0], post[1])

        def ffn_slab(l, j, acnt, post=None, ft=0):
            wib, wob = FFN_SLOT[j % 2]
            nfc = ffn_nfc(j)
            rws = ffn_ranges(j)
            for i in range(ft, 5):
                t0, t1 = TT[i]
                N = t1 - t0
                par = acnt[0] % 2
                acnt[0] += 1
                ab = FFN_ACT + par * 2048
                if post is not None and i > ft:
                    norm_a(i - 1)
                for fc in range(nfc):
                    pbg, pbu = newps(), newps()
                    for gu, pb in ((0, pbg), (1, pbu)):
                        for kc in range(8):
                            o_ = wib + kc * 1024 + gu * 512 + fc * 128
                            mm(PS[pb][:, 0:N], MIX[:, o_:o_ + 128], H.ap(kc, t0, t1), kc == 0, kc == 7,
                               rws + [H.r(kc, t0, t1)], [psr(pb)], kc == 7)
                    sb_ = 1024 + (fc % 2) * 512
                    sl, rsl = fw(sb_, sb_ + N)
                    act(sl, PS[pbg][:, 0:N], AF.Silu, [psr(pbg)], [rsl])
                    tt('dve', MIX[:, ab + fc * 512:ab + fc * 512 + N], sl, PS[pbu][:, 0:N], ALU.mult,
                       [rsl, psr(pbu)], [('MIX', ab + fc * 512, ab + fc * 512 + 512)])
                if post is not None and i > ft:
                    norm_b(i - 1)
                    norm_c1(i - 1)
                for n in range(8):
                    pb = newps()
                    for fc in range(nfc):
                        o_ = wob + fc * 1024 + n * 128
                        mm(PS[pb][:, 0:N], MIX[:, o_:o_ + 128], MIX[:, ab + fc * 512:ab + fc * 512 + N],
                           fc == 0, fc == nfc - 1, rws + [('MIX', ab + fc * 512, ab + fc * 512 + 512)],
                           [psr(pb)], fc == nfc - 1)
                    tt('dve', X.ap(n, t0, t1), X.ap(n, t0, t1), PS[pb][:, 0:N], ALU.add,
                       [X.r(n, t0, t1), psr(pb)], [X.r(n, t0, t1)])
                if post is not None and i > ft:
                    norm_c2(i - 1, post[0], post[1])
            if post is not None:
                norm_a(4)
                norm_b(4)
                norm_c1(4)
                norm_c2(4, post[0], post[1])

        def dump(name, view, nchunk, tlen):
            if name not in dbg_d:
                return
            for a in range(nchunk):
                for c0 in range(0, tlen, 512):
                    c1 = min(tlen, c0 + 512)
                    stg, rst = fw(1024, 1024 + (c1 - c0))
                    cp('dve', stg, view.ap(a, c0, c1), [view.r(a, c0, c1)], [rst])
                    dst = dbg_d[name][:, a * tlen + c0:a * tlen + c1]
                    P.dma('sp', lambda e, dst=dst, stg=stg: e.dma_start(out=dst, in_=stg), 'dbg', [rst],
                          [('dbg', 0, CELL)])

        npb = {}

        def norm_a(i):
            t0, t1 = TT[i]
            N = t1 - t0
            for kc in range(8):
                sq, rsq = bw(kc * 512, kc * 512 + N)
                act(sq, X.ap(kc, t0, t1), AF.Square, [X.r(kc, t0, t1)], [rsq])

        def norm_b(i):
            t0, t1 = TT[i]
            N = t1 - t0
            pb = newps()
            npb[i] = pb
            for kc in range(8):
                sq, rsq = bw(kc * 512, kc * 512 + N)
                mm(PS[pb][:, 0:N], NEGONES, sq, kc == 0, kc == 7, [rsq, r_cst], [psr(pb)], kc == 7)

        def norm_c1(i):
            t0, t1 = TT[i]
            N = t1 - t0
            pb = npb[i]
            lnv, rln = fw(0, N)
            rstd, rrs = fw(512, 512 + N)
            act(lnv, PS[pb][:, 0:N], AF.Ln, [psr(pb)], [rln], bias=EPS, scale=-1.0 / D)
            act(rstd, lnv, AF.Exp, [rln], [rrs], scale=-0.5)

        def norm_c2(i, gcol, final=False):
            t0, t1 = TT[i]
            N = t1 - t0
            rstd, rrs = fw(512, 512 + N)
            if not final:
                for kc in range(8):
                    stt(H.ap(kc, t0, t1), X.ap(kc, t0, t1), PRM[:, gcol + kc:gcol + kc + 1], rstd,
                        ALU.mult, ALU.mult, [X.r(kc, t0, t1), rrs, ('PRM', 0, NPC)], [H.r(kc, t0, t1)])
                return
            if i == 0:
                return
            for kc in range(8):
                lo = 4096 + (kc % 4) * 1024
                stg = BW[:, lo:lo + 2 * N].bitcast(F32)
                rst = ('BW', lo, lo + 1024)
                stt(stg, X.ap(kc, t0, t1), PRM[:, PC_FN + kc:PC_FN + kc + 1], rstd, ALU.mult, ALU.mult,
                    [X.r(kc, t0, t1), rrs, ('PRM', 0, NPC)], [rst])
                dst = y_d[kc * 128:(kc + 1) * 128, t0 - 16:t1 - 16]
                P.dma('sp', lambda e, dst=dst, stg=stg: e.dma_start(out=dst, in_=stg), 'out%d' % (kc % 4), [rst],
                      [('out', 0, CELL)])

        load_pmx(0)
        load_in_slab(0, 0, 0)
        for l in range(depth):
            load_in_slab(l, 1536, 2)
            if l == 0:
                for i in range(5):
                    norm_to_h(i, PC_N1 + l * 8)
                dump('h', H, 8, NT)
            ft = 1 if l == depth - 1 else 0
            ecnt = [0]
            if ft == 0:
                proj_fm(l, 0, 0, 'q')
            proj_fm(l, 1, 0, 'q')
            for i in range(5):
                proj_u(l, i, 2, ecnt)
                if i + 2 < 5:
                    proj_fm(l, i + 2, 0, 'q')
                if i == 2:
                    load_in_slab(l, 512, 0)
                if i >= ft:
                    pool_band(l, i)
            load_in_slab(l, 1024, 2)
            for i in range(2):
                proj_fm(l, i, 0, 'k')
            for i in range(2):
                proj_v(l, i, 2)
            work = {}
            for i in range(1, 4):
                work[i] = proj_fm_items(l, i + 1, 0, 'k', eng='dve') + proj_v_items(l, i + 1, 2)
            if l == 0:
                dump('q', Q, 4, NT)
            attention(work, after_work=lambda l=l: load_merge_slab(l, 0), ft=ft)
            if l == 0:
                dump('s', Q, 4, NT)
                dump('k', K, 4, NT)
                dump('v', V, 17, 512)
                dump('a', A, 4, NT)
            sgcnt = [0]
            for j in range(4):
                if j + 1 < 4:
                    load_merge_slab(l, j + 1)
                merge_slab(l, j, sgcnt, ft)
                if j == 2:
                    load_out_slab(l, 0)
            if l == 0:
                dump('mg', MG, 8, NT)
            load_out_slab(l, 1)
            load_ffn_slab(l, 0)
            out_proj(l, (PC_N2 + l * 8, False), ft)
            if l == 0:
                dump('x1', X, 8, NT)
            acnt = [0]
            for j in range(6):
                if j + 1 < 6:
                    load_ffn_slab(l, j + 1)
                elif l + 1 < depth:
                    load_pmx(l + 1)
                    load_in_slab(l + 1, 0, 0)
                if j < 5:
                    ffn_slab(l, j, acnt, ft=ft)
                elif l + 1 < depth:
                    ffn_slab(l, j, acnt, post=(PC_N1 + (l + 1) * 8, False))
                else:
                    ffn_slab(l, j, acnt, post=(PC_FN, True), ft=ft)
            if l == 0:
                dump('x2', X, 8, NT)
        P.op('sp', None, reads=[('out', 0, CELL), ('dbg', 0, CELL)])
        P.emit()
    return nc


def host_consts():
    c = np.zeros((128, NCST), np.float32)
    j = np.arange(128)[:, None]
    s = np.arange(128)[None, :]
    c[:, 0:128] = np.eye(128, dtype=np.float32)
    c[:, 128:256] = -1.0 * (j >= s)
    c[:, 256:384] = np.where(j >= s, NEG, 0.0)
    t = np.arange(128)[:, None]
    tp = np.arange(128)[None, :]
    for g in range(4):
        w = 2 << g
        gb = 384 + g * 400
        d = tp - t
        c[:, gb:gb + 128] = np.where((d >= 0) & (d <= w - 1), 1.0 / w, 0.0) - (d == 0)
        c[:, gb + 128:gb + 256] = np.where((128 + tp) - t <= w - 1, 1.0 / w, 0.0)
        t16 = np.arange(16)[:, None]
        tp16 = np.arange(16)[None, :]
        d16 = tp16 - t16
        cnt = np.minimum(tp16 + 1, w).astype(np.float32)
        c[0:16, gb + 256:gb + 272] = np.where((d16 >= 0) & (d16 <= w - 1), 1.0 / cnt, 0.0) - (d16 == 0)
        c[0:16, gb + 272:gb + 400] = np.where((16 + tp) - t16 <= w - 1, 1.0 / w, 0.0)
    return c


def host_params(norm1_g, norm2_g, final_norm_g, b_gate, pool_scale):
    p = np.zeros((128, NPC), np.float32)
    for l in range(DEPTH):
        p[:, PC_N1 + l * 8:PC_N1 + l * 8 + 8] = np.asarray(norm1_g[l]).reshape(8, 128).T
        p[:, PC_N2 + l * 8:PC_N2 + l * 8 + 8] = np.asarray(norm2_g[l]).reshape(8, 128).T
        p[:, PC_BG + l * 16:PC_BG + l * 16 + 16] = np.asarray(b_gate[l]).reshape(16, 128).T
        p[:, PC_PS + l * 4:PC_PS + l * 4 + 4] = np.asarray(pool_scale[l]).reshape(4, 128).T
    p[:, PC_FN:PC_FN + 8] = np.asarray(final_norm_g).reshape(8, 128).T
    return p


def make_in_maps(x, meta_tokens, norm1_g, w_in, b_gate, pool_mix, pool_scale, w_branch_pool,
                 w_branch_sb, w_out, norm2_g, w_ffn_in, w_ffn_out, final_norm_g):
    f = lambda a: np.ascontiguousarray(np.asarray(a, dtype=np.float32))
    x = f(x)
    shared = dict(
        metaT=f(np.asarray(meta_tokens, np.float32).T),
        params=host_params(norm1_g, norm2_g, final_norm_g, b_gate, pool_scale),
        consts=host_consts(),
        w_in=f(w_in), pool_mix=f(pool_mix), w_branch_pool=f(w_branch_pool), w_branch_sb=f(w_branch_sb),
        w_out=f(w_out), w_ffn_in=f(w_ffn_in), w_ffn_out=f(w_ffn_out),
    )
    maps = []
    for b in range(x.shape[0]):
        m = dict(shared)
        m["xT"] = f(x[b].T)
        maps.append(m)
    return maps


_NC_CACHE = {}


def kernel(x, meta_tokens, norm1_g, w_in, b_gate, pool_mix, pool_scale, w_branch_pool,
           w_branch_sb, w_out, norm2_g, w_ffn_in, w_ffn_out, final_norm_g):
    in_maps = make_in_maps(x, meta_tokens, norm1_g, w_in, b_gate, pool_mix, pool_scale, w_branch_pool,
                           w_branch_sb, w_out, norm2_g, w_ffn_in, w_ffn_out, final_norm_g)
    nc = build()
    res = run_bass_kernel_spmd(nc, in_maps, core_ids=list(range(8)))
    out = np.stack([np.ascontiguousarray(r["yT"].T) for r in res.results], axis=0)
    return out.astype(np.float32)
```

```python
import contextlib
import numpy as np
import concourse.bass as bass
import concourse.mybir as mybir
from concourse.bass_utils import run_bass_kernel_spmd

F32 = mybir.dt.float32
BF16 = mybir.dt.bfloat16
AF = mybir.ActivationFunctionType
ALU = mybir.AluOpType

D = 1024
SEQ = 2048
NMETA = 16
NT = SEQ + NMETA
DFF = 2816
DEPTH = 2
EPS = 1e-6
TT = [(0, 16)] + [(16 + 512 * i, 16 + 512 * (i + 1)) for i in range(4)]
KB = [(0, 16)] + [(16 + 128 * j, 144 + 128 * j) for j in range(16)]
CELL = 16
NEG = -30000.0
GATT = 5
FLAGS = dict(av_tp00=True, mask_after=True, pair_act=True)

PC_N1 = 0
PC_N2 = 16
PC_FN = 32
PC_BG = 40
PC_PS = 72
NPC = 80
NBAND = 1600
NCST = 384 + NBAND


class Prog:
    def __init__(self, nc):
        self.nc = nc
        self.ops = []
        self.cnt = {}
        self.arenas = {}

    def arena(self, name, nelem):
        self.arenas[name] = dict(n=(nelem + CELL - 1) // CELL, W={}, R={})

    def _record(self, eng, fn, reads, writes, stream, val, skip):
        need = {}
        for (an, lo, hi) in reads:
            A = self.arenas[an]
            c0, c1 = lo // CELL, (hi + CELL - 1) // CELL
            for s, arr in A['W'].items():
                v = int(arr[c0:c1].max())
                if v > need.get(s, 0):
                    need[s] = v
        for (an, lo, hi) in writes:
            A = self.arenas[an]
            c0, c1 = lo // CELL, (hi + CELL - 1) // CELL
            for tab in (A['W'], A['R']):
                for s, arr in tab.items():
                    v = int(arr[c0:c1].max())
                    if v > need.get(s, 0):
                        need[s] = v
        for s in skip:
            need.pop(s, None)
        for s, v in need.items():
            if s.startswith('E:'):
                assert v <= self.cnt.get(s, 0), ("dependency on a not-yet-issued inc", s, v)
        for (an, lo, hi) in reads:
            A = self.arenas[an]
            c0, c1 = lo // CELL, (hi + CELL - 1) // CELL
            if stream not in A['R']:
                A['R'][stream] = np.zeros(A['n'], np.int64)
            A['R'][stream][c0:c1] = val
        for (an, lo, hi) in writes:
            A = self.arenas[an]
            c0, c1 = lo // CELL, (hi + CELL - 1) // CELL
            if stream not in A['W']:
                A['W'][stream] = np.zeros(A['n'], np.int64)
            A['W'][stream][c0:c1] = val
        return need

    def op(self, eng, fn, reads=(), writes=(), inc=True):
        stream = 'E:' + eng
        val = self.cnt.get(stream, 0) + 1
        skip = (stream,) if eng == 'pe' else ()
        need = self._record(eng, fn, reads, writes, stream, val, skip)
        if inc:
            self.cnt[stream] = val
        self.ops.append(dict(eng=eng, fn=fn, need=need, inc=inc, dma=None))

    def dma(self, queue, fn, sem, reads=(), writes=()):
        stream = 'D:' + sem
        val = self.cnt.get(stream, 0) + 16
        need = self._record(queue, fn, reads, writes, stream, val, (stream,))
        self.cnt[stream] = val
        self.ops.append(dict(eng=queue, fn=fn, need=need, inc=True, dma=sem))

    def emit(self):
        nc = self.nc
        ops = self.ops
        sem_names = [s for s in self.cnt if self.cnt[s] > 0]
        with contextlib.ExitStack() as st:
            sems = {}
            for n in sem_names:
                sems[n] = st.enter_context(nc.semaphore(n.replace(':', '_')))
            block = st.enter_context(nc.Block())

            def run(ename, eng):
                waited = {}
                for o in ops:
                    if o['eng'] != ename:
                        continue
                    for s, v in o['need'].items():
                        if waited.get(s, 0) < v:
                            eng.wait_ge(sems[s], v)
                            waited[s] = v
                    if o['fn'] is None:
                        continue
                    ins = o['fn'](eng)
                    if o['dma'] is not None:
                        ins.then_inc(sems['D:' + o['dma']], 16)
                    elif o['inc']:
                        ins.then_inc(sems['E:' + ename], 1)

            used = set(o['eng'] for o in ops)
            if 'pe' in used:
                @block.tensor
                def _(e):
                    run('pe', e)
            if 'act' in used:
                @block.scalar
                def _(e):
                    run('act', e)
            if 'dve' in used:
                @block.vector
                def _(e):
                    run('dve', e)
            if 'pool' in used:
                @block.gpsimd
                def _(e):
                    run('pool', e)
            if 'sp' in used:
                @block.sync
                def _(e):
                    run('sp', e)


class View:
    def __init__(self, tens, name, off, A, T):
        self.tens, self.name, self.off, self.A, self.T = tens, name, off, A, T

    def ap(self, a, t0, t1, p0=0, p1=128):
        b = self.off + a * self.T
        return self.tens[p0:p1, b + t0:b + t1]

    def r(self, a, t0, t1):
        b = self.off + a * self.T
        return (self.name, b + t0, b + t1)


def build(depth=DEPTH, dbg=None):
    nc = bass.Bass("TRN2", target_bir_lowering=False)
    dt = nc.dram_tensor
    xT_d = dt("xT", [D, SEQ], F32, kind="ExternalInput").ap()
    meta_d = dt("metaT", [D, NMETA], F32, kind="ExternalInput").ap()
    prm_d = dt("params", [128, NPC], F32, kind="ExternalInput").ap()
    cst_d = dt("consts", [128, NCST], F32, kind="ExternalInput").ap()
    w_in_d = dt("w_in", [DEPTH, D, 4096], F32, kind="ExternalInput").ap()
    pmix_d = dt("pool_mix", [DEPTH, 4, 128, 128], F32, kind="ExternalInput").ap()
    wbp_d = dt("w_branch_pool", [DEPTH, 512, D], F32, kind="ExternalInput").ap()
    wbs_d = dt("w_branch_sb", [DEPTH, 512, D], F32, kind="ExternalInput").ap()
    wout_d = dt("w_out", [DEPTH, D, D], F32, kind="ExternalInput").ap()
    wfi_d = dt("w_ffn_in", [DEPTH, D, 2 * DFF], F32, kind="ExternalInput").ap()
    wfo_d = dt("w_ffn_out", [DEPTH, DFF, D], F32, kind="ExternalInput").ap()
    y_d = dt("yT", [D, SEQ], F32, kind="ExternalOutput").ap()
    dbg_d = {}
    if dbg:
        for name, shape in dbg.items():
            dbg_d[name] = dt("dbg_" + name, shape, F32, kind="ExternalOutput").ap()

    P = Prog(nc)
    with contextlib.ExitStack() as st:
        def sb(name, n, dtype):
            P.arena(name, n)
            return st.enter_context(nc.sbuf_tensor(name, [128, n], dtype))

        Xt = sb("X", 8 * NT, F32)
        PRM = sb("PRM", NPC, F32)
        FW = sb("FW", 2048, F32)
        Ht = sb("H", 8 * NT, BF16)
        MIX = sb("MIX", 33472, BF16)
        WB = sb("WB", 4 * 2048, BF16)
        CST = sb("CST", 512 + NBAND, BF16)
        BW = sb("BW", 8192, BF16)
        ZER = sb("ZER", 128, BF16)
        PMX = sb("PMX", 512, BF16)
        PSB = [st.enter_context(nc.psum_tensor("psb%d" % b, [128, 1024], F32)) for b in range(4)]

        class _Bank:
            def __init__(self, b):
                self.t, self.o = PSB[b // 2], (b % 2) * 512

            def __getitem__(self, idx):
                ps_, cs_ = idx
                c0 = cs_.start or 0
                c1 = 512 if cs_.stop is None else cs_.stop
                return self.t[ps_, self.o + c0:self.o + c1]

        PS = [_Bank(b) for b in range(8)]
        P.arena('ps', 8 * CELL)
        P.arena('out', CELL)
        P.arena('dbg', CELL)

        X = View(Xt, "X", 0, 8, NT)
        H = View(Ht, "H", 0, 8, NT)
        Q = View(MIX, "MIX", 0, 4, NT)
        K = View(MIX, "MIX", 8256, 4, NT)
        V = View(MIX, "MIX", 16512, 17, 512)
        A = View(MIX, "MIX", 25216, 4, NT)
        MG = View(MIX, "MIX", 8256, 8, NT)

        def psr(b):
            return ('ps', b * CELL, (b + 1) * CELL)

        def fw(lo, hi):
            return FW[:, lo:hi], ('FW', lo, hi)

        def bw(lo, hi):
            return BW[:, lo:hi], ('BW', lo, hi)

        IDENT = CST[:, 0:128]
        NEGU = CST[:, 128:256]
        NEGM = CST[:, 256:384]
        NEGONES = CST[:, 384:512]
        r_cst = ('CST', 0, 512)

        psn = [0]
        ps_pool = [list(range(8))]

        def newps():
            pool_ = ps_pool[0]
            b = pool_[psn[0] % len(pool_)]
            psn[0] += 1
            return b

        def mm(out, lhsT, rhs, start, stop, reads, writes, inc, tp=None):
            if tp is None:
                P.op('pe', lambda e: e.matmul(out, lhsT=lhsT, rhs=rhs, start=start, stop=stop),
                     reads, writes, inc)
            else:
                P.op('pe', lambda e: e.matmul(out, lhsT=lhsT, rhs=rhs, start=start, stop=stop,
                                              tile_position=tp), reads, writes, inc)

        def mmx(out, lhsT, rhs, start, stop, reads, writes, inc):
            P.op('pe', lambda e: e.matmul(out, lhsT=lhsT, rhs=rhs, start=start, stop=stop,
                                          skip_group_check=True), reads, writes, inc)

        def act(out, in_, func, reads, writes, bias=None, scale=None):
            kw = {}
            if bias is not None:
                kw['bias'] = bias
            if scale is not None:
                kw['scale'] = scale
            P.op('act', lambda e: e.activation(out=out, in_=in_, func=func, **kw), reads, writes)

        def tt(eng, out, in0, in1, op, reads, writes):
            P.op(eng, lambda e: e.tensor_tensor(out=out, in0=in0, in1=in1, op=op), reads, writes)

        def stt(out, in0, scalar, in1, op0, op1, reads, writes):
            P.op('dve', lambda e: e.scalar_tensor_tensor(out=out, in0=in0, scalar=scalar, in1=in1,
                                                         op0=op0, op1=op1), reads, writes)

        def cp(eng, out, in_, reads, writes):
            if eng == 'act':
                P.op(eng, lambda e: e.copy(out=out, in_=in_), reads, writes)
            else:
                P.op(eng, lambda e: e.tensor_copy(out=out, in_=in_), reads, writes)

        def memset(eng, ap, val, writes):
            P.op(eng, lambda e: e.memset(ap, val), (), writes)

        def wdma(out, in_, sem, writes):
            P.dma('pool', lambda e: e.dma_start(out=out, in_=in_), sem, (), writes)

        P.dma('sp', lambda e: e.dma_start(out=PRM[:, :], in_=prm_d), 'prm', (), [('PRM', 0, NPC)])
        wdma(CST[:, 0:384], cst_d[:, 0:384], 'cst', [('CST', 0, 384)])
        wdma(CST[:, 512:512 + NBAND], cst_d[:, 384:NCST], 'cst2', [('CST', 512, 512 + NBAND)])
        memset('dve', CST[:, 384:512], -1.0, [('CST', 384, 512)])
        memset('dve', ZER[:, :], 0.0, [('ZER', 0, 128)])
        X3 = Xt[:, :].rearrange("p (k t) -> p k t", t=NT)
        P.dma('sp', lambda e: e.dma_start(out=X3[:, :, 0:16], in_=meta_d.rearrange("(k p) t -> p k t", p=128)),
              'x0', (), [X.r(kc, 0, 16) for kc in range(8)])
        for i in range(1, 5):
            t0, t1 = TT[i]
            src = xT_d[:, t0 - 16:t1 - 16].rearrange("(k p) t -> p k t", p=128)
            dst = X3[:, :, t0:t1]
            P.dma('sp', lambda e, dst=dst, src=src: e.dma_start(out=dst, in_=src),
                  'x%d' % i, (), [X.r(kc, t0, t1) for kc in range(8)])

        def wunit(u):
            if u < 4:
                return WB, 'WB', u * 2048
            return BW, 'BW', (u - 4) * 2048

        def load_in_slab(l, col0, u0):
            off = u0 * 2048
            dst = WB[:, off:off + 4096].rearrange("p (k n) -> p k n", n=512)
            src = w_in_d[l, :, col0:col0 + 512].rearrange("(k p) n -> p k n", p=128)
            wdma(dst, src, 'wu%d' % u0, [('WB', off, off + 4096)])

        def load_merge_slab(l, j):
            us = (0, 1, 2) if j % 2 == 0 else (3, 4, 5)
            for idx, col0 in ((0, 2048 + 256 * j), (1, 3072 + 256 * j)):
                T_, an, off = wunit(us[idx])
                dst = T_[:, off:off + 2048].rearrange("p (k n) -> p k n", n=256)
                src = w_in_d[l, :, col0:col0 + 256].rearrange("(k p) n -> p k n", p=128)
                wdma(dst, src, 'wu%d' % us[idx], [(an, off, off + 2048)])
            T_, an, off = wunit(us[2])
            for idx, wd in ((0, wbp_d), (1, wbs_d)):
                dst = T_[:, off + idx * 1024:off + idx * 1024 + 1024].rearrange("p (k n) -> p k n", n=256)
                src = wd[l, :, 256 * j:256 * j + 256].rearrange("(k p) n -> p k n", p=128)
                wdma(dst, src, 'wu%d' % us[2], [(an, off, off + 2048)])

        def load_out_slab(l, jj):
            off = jj * 4096
            dst = WB[:, off:off + 4096].rearrange("p (k n) -> p k n", n=512)
            src = wout_d[l, :, 512 * jj:512 * jj + 512].rearrange("(k p) n -> p k n", p=128)
            wdma(dst, src, 'wu%d' % (2 * jj), [('WB', off, off + 4096)])

        FFN_SLABS = [(0, 2), (2, 4), (6, 4), (10, 4), (14, 4), (18, 4)]

        def ffn_nfc(j):
            return FFN_SLABS[j][1]

        FFN_SLOT = [(0, 25216), (8256, 16448)]
        FFN_ACT = 29312

        def ffn_ranges(j):
            wib, wob = FFN_SLOT[j % 2]
            return [('MIX', wib, wib + 8192), ('MIX', wob, wob + 4096)]

        def load_ffn_slab(l, j):
            wib, wob = FFN_SLOT[j % 2]
            nfc = ffn_nfc(j)
            nc_ = 128 * nfc
            wi4 = MIX[:, wib:wib + 8192].rearrange("p (k g c) -> p k g c", k=8, g=2)
            sem = 'f%d' % (j % 2)
            for gu in range(2):
                fo = FFN_SLABS[j][0] * 128
                src = wfi_d[l, :, gu * DFF + fo:gu * DFF + fo + nc_].rearrange("(k p) n -> p k n", p=128)
                wdma(wi4[:, :, gu, 0:nc_], src, sem, ffn_ranges(j))
            dst = MIX[:, wob:wob + nfc * 1024].rearrange("p (f n) -> p f n", n=1024)
            fo = FFN_SLABS[j][0] * 128
            src = wfo_d[l, fo:fo + nc_, :].rearrange("(f p) n -> p f n", p=128)
            wdma(dst, src, sem, ffn_ranges(j))

        def load_pmx(l):
            dst = PMX[:, :].rearrange("p (g d) -> p g d", d=128)
            src = pmix_d[l].rearrange("g c d -> c g d")
            wdma(dst, src, 'pmx', [('PMX', 0, 512)])

        def norm_stats(i):
            t0, t1 = TT[i]
            N = t1 - t0
            pb = newps()
            for kc in range(8):
                sq, rsq = bw((kc % 2) * 512, (kc % 2) * 512 + N)
                act(sq, X.ap(kc, t0, t1), AF.Square, [X.r(kc, t0, t1)], [rsq])
                mm(PS[pb][:, 0:N], NEGONES, sq, kc == 0, kc == 7, [rsq, r_cst], [psr(pb)], True)
            lnv, rln = fw(0, N)
            rstd, rrs = fw(512, 512 + N)
            act(lnv, PS[pb][:, 0:N], AF.Ln, [psr(pb)], [rln], bias=EPS, scale=-1.0 / D)
            act(rstd, lnv, AF.Exp, [rln], [rrs], scale=-0.5)
            return rstd, rrs

        def norm_to_h(i, gcol):
            t0, t1 = TT[i]
            rstd, rrs = norm_stats(i)
            for kc in range(8):
                stt(H.ap(kc, t0, t1), X.ap(kc, t0, t1), PRM[:, gcol + kc:gcol + kc + 1], rstd,
                    ALU.mult, ALU.mult, [X.r(kc, t0, t1), rrs, ('PRM', 0, NPC)], [H.r(kc, t0, t1)])

        def tile_kbs(i):
            return [0] if i == 0 else list(range(4 * (i - 1) + 1, 4 * i + 1))

        r_band = ('CST', 512, 512 + NBAND)

        def proj_u(l, i, u0, ecnt):
            off = u0 * 2048
            rw = ('WB', off, off + 4096)
            for kb in tile_kbs(i):
                b0, b1 = KB[kb]
                nk = b1 - b0
                pb = newps()
                for kc in range(8):
                    mm(PS[pb][0:nk, 0:512], H.ap(kc, b0, b1), WB[:, off + kc * 512:off + kc * 512 + 512],
                       kc == 0, kc == 7, [rw, H.r(kc, b0, b1)], [psr(pb)], kc == 7)
                ub = 2048 + (kb % 6) * 512
                eng = 'act' if ecnt[0] % 2 == 0 else 'dve'
                ecnt[0] += 1
                cp(eng, BW[0:nk, ub:ub + 512], PS[pb][0:nk, 0:512], [psr(pb)], [('BW', ub, ub + 512)])

        def pool_band(l, i):
            t0, t1 = TT[i]
            N = t1 - t0
            kbs = tile_kbs(i)
            pend = []

            def band(g):
                pd = newps()
                gb = 512 + g * 400
                for j, kb in enumerate(kbs):
                    b0, b1 = KB[kb]
                    nk = b1 - b0
                    ub = 2048 + (kb % 6) * 512
                    c0 = j * 128
                    if kb == 0:
                        mmx(PS[pd][:, c0:c0 + nk], BW[0:16, ub + g * 128:ub + g * 128 + 128],
                            CST[0:16, gb + 256:gb + 272], True, True, [('BW', ub, ub + 512), r_band], [psr(pd)],
                            j == len(kbs) - 1)
                        continue
                    mmx(PS[pd][:, c0:c0 + nk], BW[:, ub + g * 128:ub + g * 128 + 128], CST[:, gb:gb + 128],
                        True, False, [('BW', ub, ub + 512), r_band], [psr(pd)], False)
                    pk = kb - 1
                    pub = 2048 + (pk % 6) * 512
                    if pk == 0:
                        mmx(PS[pd][:, c0:c0 + nk], BW[0:16, pub + g * 128:pub + g * 128 + 128],
                            CST[0:16, gb + 272:gb + 400], False, True, [('BW', pub, pub + 512), r_band],
                            [psr(pd)], j == len(kbs) - 1)
                    else:
                        mmx(PS[pd][:, c0:c0 + nk], BW[:, pub + g * 128:pub + g * 128 + 128],
                            CST[:, gb + 128:gb + 256], False, True, [('BW', pub, pub + 512), r_band],
                            [psr(pd)], j == len(kbs) - 1)
                db = 1024 + (g % 2) * 512
                cp('dve', BW[:, db:db + N], PS[pd][:, 0:N], [psr(pd)], [('BW', db, db + 512)])
                return db

            def mix(g, db):
                pb2 = newps()
                mm(PS[pb2][:, 0:N], PMX[:, g * 128:(g + 1) * 128], BW[:, db:db + N], True, True,
                   [('BW', db, db + 512), ('PMX', 0, 512)], [psr(pb2)], True)
                pc = PC_PS + l * 4 + g
                act(A.ap(g, t0, t1), PS[pb2][:, 0:N], AF.Identity, [psr(pb2), ('PRM', 0, NPC)], [A.r(g, t0, t1)],
                    scale=PRM[:, pc:pc + 1])

            d0 = band(0)
            d1 = band(1)
            mix(0, d0)
            d2 = band(2)
            mix(1, d1)
            d3 = band(3)
            mix(2, d2)
            mix(3, d3)

        def proj_fm_items(l, i, u0, kind, eng='act'):
            t0, t1 = TT[i]
            N = t1 - t0
            off = u0 * 2048
            rw = ('WB', off, off + 4096)
            items = []
            for n in range(4):
                def item(n=n):
                    pb = newps()
                    for kc in range(8):
                        mm(PS[pb][:, 0:N], WB[:, off + kc * 512 + n * 128:off + kc * 512 + n * 128 + 128],
                           H.ap(kc, t0, t1), kc == 0, kc == 7, [rw, H.r(kc, t0, t1)], [psr(pb)], kc == 7)
                    if kind == 'k':
                        cp(eng, K.ap(n, t0, t1), PS[pb][:, 0:N], [psr(pb)], [K.r(n, t0, t1)])
                    else:
                        act(Q.ap(n, t0, t1), PS[pb][:, 0:N], AF.Identity, [psr(pb)], [Q.r(n, t0, t1)], scale=0.125)
                items.append(item)
            return items

        def proj_fm(l, i, u0, kind):
            for it in proj_fm_items(l, i, u0, kind):
                it()

        def proj_v_items(l, i, u0):
            off = u0 * 2048
            rw = ('WB', off, off + 4096)
            items = []
            for kb in tile_kbs(i):
                def item(kb=kb):
                    b0, b1 = KB[kb]
                    nk = b1 - b0
                    pb = newps()
                    for kc in range(8):
                        mm(PS[pb][0:nk, 0:512], H.ap(kc, b0, b1), WB[:, off + kc * 512:off + kc * 512 + 512],
                           kc == 0, kc == 7, [rw, H.r(kc, b0, b1)], [psr(pb)], kc == 7)
                    cp('dve', V.ap(kb, 0, 512, 0, nk), PS[pb][0:nk, 0:512], [psr(pb)], [V.r(kb, 0, 512)])
                items.append(item)
            return items

        def proj_v(l, i, u0):
            for it in proj_v_items(l, i, u0):
                it()

        def attention(work=None, after_work=None, ft=0):
            PO = 6
            ps_pool[0] = [7]
            SB_S = GATT * 1024
            assert GATT <= 5
            zc = [0]
            wc = [0]
            for i in range(ft, 5):
                t0, t1 = TT[i]
                N = t1 - t0
                kbs = [0] if i == 0 else list(range(4 * i, -1, -1))
                first_diag = 0 if i == 0 else 4 * (i - 1) + 1
                n = len(kbs)

                def geom(si):
                    kb = kbs[si]
                    b0, b1 = KB[kb]
                    diag = kb >= first_diag
                    c0 = (kb - first_diag) * 128 if (diag and i > 0) else 0
                    return kb, b0, b1, b1 - b0, diag, c0

                def pair(T_, base, nk, c0):
                    return T_[0:nk, base:base + 1024].rearrange("p (h c) -> p h c", h=2)[:, :, c0:N]

                def qk(hp, si, r):
                    kb, b0, b1, nk, diag, c0 = geom(si)
                    dw = min(128, N - c0)
                    for hh in range(2):
                        pz = 2 * r + hh
                        p0, p1 = 64 * hh, 64 * hh + 64
                        mmx(PS[pz][0:nk, c0:N], K.ap(hp, b0, b1, p0, p1), Q.ap(hp, t0 + c0, t1, p0, p1),
                            True, not diag, [K.r(hp, b0, b1), Q.r(hp, t0 + c0, t1)], [psr(pz)], not diag)
                    if diag:
                        for hh in range(2):
                            pz = 2 * r + hh
                            mmx(PS[pz][0:nk, c0:c0 + dw], CST[0:nk, 0:nk], CST[0:nk, 256:256 + dw],
                                False, True, [r_cst], [psr(pz)], True)

                def prep(a):
                    kind, hp, si, gi = a['kind'], a['hp'], a['si'], a['gi']
                    kb, b0, b1, nk, diag, c0 = geom(si)
                    r = zc[0] % 3
                    zc[0] += 1
                    a['r'] = r
                    qk(hp, si, r)
                    if kind == 'ex':
                        sp_r = ('BW', gi * 1024, gi * 1024 + 1024)
                        so = SB_S + (si % 2) * 1024
                        sn = SB_S + ((si + 1) % 2) * 1024
                        s_r = ('BW', so, so + 1024)
                        if si == 0 and n > 1:
                            memset('dve', BW[:, SB_S:SB_S + 2048], 0.0, [('BW', SB_S, SB_S + 2048)])
                        for hh in range(2):
                            pz = 2 * r + hh
                            sp_ap = BW[0:nk, gi * 1024 + hh * 512 + c0:gi * 1024 + hh * 512 + N]
                            mmx(PS[pz][0:nk, c0:N], CST[0:nk, 128:128 + nk], sp_ap, False, si == 0,
                                [r_cst, sp_r], [psr(pz)], si == 0)
                            if si > 0:
                                mmx(PS[pz][0:nk, c0:N], CST[:, 384:384 + nk],
                                    BW[:, so + hh * 512 + c0:so + hh * 512 + N], False, True,
                                    [r_cst, s_r], [psr(pz)], True)
                        if si < n - 1:
                            tt('dve', pair(BW, sn, 128, c0), pair(BW, so, 128, c0), pair(BW, gi * 1024, 128, c0),
                               ALU.add, [s_r, sp_r], [('BW', sn, sn + 1024)])

                def wview(gi, nk):
                    if gi < 4:
                        return FW[0:nk, gi * 512:(gi + 1) * 512].bitcast(BF16), ('FW', gi * 512, gi * 512 + 512)
                    return BW[0:nk, 7168:8192], ('BW', 7168, 8192)

                def actop(a):
                    kind, hp, si, gi, r = a['kind'], a['hp'], a['si'], a['gi'], a['r']
                    kb, b0, b1, nk, diag, c0 = geom(si)
                    if kind == 'sp':
                        act(pair(BW, gi * 1024, nk, c0), pair(PSB[r], 0, nk, c0), AF.Softplus,
                            [psr(2 * r), psr(2 * r + 1)], [('BW', gi * 1024, gi * 1024 + 1024)])
                    else:
                        wv, w_r = wview(gi, nk)
                        act(wv.rearrange("p (h c) -> p h c", h=2)[:, :, c0:N], pair(PSB[r], 0, nk, c0), AF.Exp,
                            [psr(2 * r), psr(2 * r + 1)], [w_r])

                pending = []

                def post(a):
                    if a['kind'] != 'ex':
                        return
                    hp, si, gi = a['hp'], a['si'], a['gi']

                    def emit_av():
                        kb, b0, b1, nk, diag, c0 = geom(si)
                        wv, w_r = wview(gi, nk)
                        if si == 0:
                            mm(PS[PO][:, 0:N], ZER[:, 0:128], CST[:, 0:N], True, False, [('ZER', 0, 128), r_cst],
                               [psr(PO)], False)
                        last = (si == n - 1)
                        for hh in range(2):
                            h = 2 * hp + hh
                            w_ap = wv[:, hh * 512 + c0:hh * 512 + N]
                            mm(PS[PO][64 * hh:64 * hh + 64, c0:N], V.ap(kb, h * 64, h * 64 + 64, 0, nk), w_ap,
                               False, last, [V.r(kb, 0, 512), w_r], [psr(PO)], hh == 1, tp=(0, 64 * hh))
                        if last:
                            cp('dve', Q.ap(hp, t0, t1), PS[PO][:, 0:N], [psr(PO)], [Q.r(hp, t0, t1)])
                    pending.append(emit_av)

                entries = [(hp, si) for hp in range(4) for si in range(n)]
                acts = []
                for j in range(0, len(entries), GATT):
                    grp = entries[j:j + GATT]
                    for gi, (hp, si) in enumerate(grp):
                        acts.append(dict(kind='sp', hp=hp, si=si, gi=gi))
                    for gi, (hp, si) in enumerate(grp):
                        acts.append(dict(kind='ex', hp=hp, si=si, gi=gi))
                for k in range(min(2, len(acts))):
                    prep(acts[k])
                for k in range(len(acts)):
                    if k + 2 < len(acts):
                        prep(acts[k + 2])
                    if acts[k]['kind'] == 'ex' and acts[k]['gi'] == 0:
                        while pending:
                            pending.pop(0)()
                    actop(acts[k])
                    post(acts[k])
                    if acts[k]['kind'] == 'sp' and pending:
                        pending.pop(0)()
                    if work is not None and acts[k]['kind'] == 'sp' and acts[k]['gi'] % 2 == 0 and work.get(i):
                        work[i].pop(0)()
                while pending:
                    pending.pop(0)()
                if work is not None:
                    while work.get(i):
                        work[i].pop(0)()
                    if after_work is not None and i == max(work):
                        after_work()
            ps_pool[0] = list(range(8))

        def merge_slab(l, j, sgcnt, ft=0):
            us = (0, 1, 2) if j % 2 == 0 else (3, 4, 5)
            Tg, ang, offg = wunit(us[0])
            Ts, ans, offs = wunit(us[1])
            Tb, anb, offb = wunit(us[2])
            for i in range(ft, 5):
                t0, t1 = TT[i]
                N = t1 - t0
                for nn in range(2):
                    n = 2 * j + nn
                    pbs = [newps() for _ in range(4)]
                    for kc in range(8):
                        mm(PS[pbs[0]][:, 0:N], Tg[:, offg + kc * 256 + nn * 128:offg + kc * 256 + nn * 128 + 128],
                           H.ap(kc, t0, t1), kc == 0, kc == 7, [(ang, offg, offg + 2048), H.r(kc, t0, t1)],
                           [psr(pbs[0])], kc == 7)
                    for kc in range(8):
                        mm(PS[pbs[1]][:, 0:N], Ts[:, offs + kc * 256 + nn * 128:offs + kc * 256 + nn * 128 + 128],
                           H.ap(kc, t0, t1), kc == 0, kc == 7, [(ans, offs, offs + 2048), H.r(kc, t0, t1)],
                           [psr(pbs[1])], kc == 7)
                    for c in range(4):
                        o_ = offb + c * 256 + nn * 128
                        mm(PS[pbs[2]][:, 0:N], Tb[:, o_:o_ + 128], A.ap(c, t0, t1), c == 0, c == 3,
                           [(anb, offb, offb + 2048), A.r(c, t0, t1)], [psr(pbs[2])], c == 3)
                    for c in range(4):
                        o_ = offb + 1024 + c * 256 + nn * 128
                        mm(PS[pbs[3]][:, 0:N], Tb[:, o_:o_ + 128], Q.ap(c, t0, t1), c == 0, c == 3,
                           [(anb, offb, offb + 2048), Q.r(c, t0, t1)], [psr(pbs[3])], c == 3)
                    k3 = sgcnt[0] % 2
                    sgcnt[0] += 1
                    g1, rg1 = fw(k3 * 1024, k3 * 1024 + N)
                    g2, rg2 = fw(k3 * 1024 + 512, k3 * 1024 + 512 + N)
                    bc = PC_BG + l * 16
                    act(g1, PS[pbs[0]][:, 0:N], AF.Sigmoid, [psr(pbs[0]), ('PRM', 0, NPC)], [rg1],
                        bias=PRM[:, bc + n:bc + n + 1])
                    act(g2, PS[pbs[1]][:, 0:N], AF.Sigmoid, [psr(pbs[1]), ('PRM', 0, NPC)], [rg2],
                        bias=PRM[:, bc + 8 + n:bc + 8 + n + 1])
                    tt('dve', g1, g1, PS[pbs[2]][:, 0:N], ALU.mult, [rg1, psr(pbs[2])], [rg1])
                    tt('dve', g2, g2, PS[pbs[3]][:, 0:N], ALU.mult, [rg2, psr(pbs[3])], [rg2])
                    tt('dve', MG.ap(n, t0, t1), g1, g2, ALU.add, [rg1, rg2], [MG.r(n, t0, t1)])

        def out_proj(l, post, ft=0):
            order = [1, 0, 2, 3, 4] if ft == 0 else [1, 2, 3, 4]
            prev = None
            for i in order:
                t0, t1 = TT[i]
                N = t1 - t0
                if prev is not None:
                    norm_a(prev)
                for n in range(8):
                    jj, nn = divmod(n, 4)
                    off = jj * 4096
                    rw = ('WB', off, off + 4096)
                    pb = newps()
                    for kc in range(8):
                        mm(PS[pb][:, 0:N], WB[:, off + kc * 512 + nn * 128:off + kc * 512 + nn * 128 + 128],
                           MG.ap(kc, t0, t1), kc == 0, kc == 7, [rw, MG.r(kc, t0, t1)], [psr(pb)], kc == 7)
                    tt('dve', X.ap(n, t0, t1), X.ap(n, t0, t1), PS[pb][:, 0:N], ALU.add,
                       [X.r(n, t0, t1), psr(pb)], [X.r(n, t0, t1)])
                    if n == 3 and prev is not None:
                        norm_b(prev)
                        norm_c1(prev)
                if prev is not None:
                    norm_c2(prev, post[0], post[1])
                prev = i
            norm_a(prev)
            norm_b(prev)
            norm_c1(prev)
            norm_c2(prev, post[0], post[1])

        def ffn_slab(l, j, acnt, post=None, ft=0):
            wib, wob = FFN_SLOT[j % 2]
            nfc = ffn_nfc(j)
            rws = ffn_ranges(j)
            for i in range(ft, 5):
                t0, t1 = TT[i]
                N = t1 - t0
                par = acnt[0] % 2
                acnt[0] += 1
                ab = FFN_ACT + par * 2048
                if post is not None and i > ft:
                    norm_a(i - 1)
                for fc in range(nfc):
                    pbg, pbu = newps(), newps()
                    for gu, pb in ((0, pbg), (1, pbu)):
                        for kc in range(8):
                            o_ = wib + kc * 1024 + gu * 512 + fc * 128
                            mm(PS[pb][:, 0:N], MIX[:, o_:o_ + 128], H.ap(kc, t0, t1), kc == 0, kc == 7,
                               rws + [H.r(kc, t0, t1)], [psr(pb)], kc == 7)
                    sb_ = 1024 + (fc % 2) * 512
                    sl, rsl = fw(sb_, sb_ + N)
                    act(sl, PS[pbg][:, 0:N], AF.Silu, [psr(pbg)], [rsl])
                    tt('dve', MIX[:, ab + fc * 512:ab + fc * 512 + N], sl, PS[pbu][:, 0:N], ALU.mult,
                       [rsl, psr(pbu)], [('MIX', ab + fc * 512, ab + fc * 512 + 512)])
                if post is not None and i > ft:
                    norm_b(i - 1)
                    norm_c1(i - 1)
                for n in range(8):
                    pb = newps()
                    for fc in range(nfc):
                        o_ = wob + fc * 1024 + n * 128
                        mm(PS[pb][:, 0:N], MIX[:, o_:o_ + 128], MIX[:, ab + fc * 512:ab + fc * 512 + N],
                           fc == 0, fc == nfc - 1, rws + [('MIX', ab + fc * 512, ab + fc * 512 + 512)],
                           [psr(pb)], fc == nfc - 1)
                    tt('dve', X.ap(n, t0, t1), X.ap(n, t0, t1), PS[pb][:, 0:N], ALU.add,
                       [X.r(n, t0, t1), psr(pb)], [X.r(n, t0, t1)])
                if post is not None and i > ft:
                    norm_c2(i - 1, post[0], post[1])
            if post is not None:
                norm_a(4)
                norm_b(4)
                norm_c1(4)
                norm_c2(4, post[0], post[1])

        def dump(name, view, nchunk, tlen):
            if name not in dbg_d:
                return
            for a in range(nchunk):
                for c0 in range(0, tlen, 512):
                    c1 = min(tlen, c0 + 512)
                    stg, rst = fw(1024, 1024 + (c1 - c0))
                    cp('dve', stg, view.ap(a, c0, c1), [view.r(a, c0, c1)], [rst])
                    dst = dbg_d[name][:, a * tlen + c0:a * tlen + c1]
                    P.dma('sp', lambda e, dst=dst, stg=stg: e.dma_start(out=dst, in_=stg), 'dbg', [rst],
                          [('dbg', 0, CELL)])

        npb = {}

        def norm_a(i):
            t0, t1 = TT[i]
            N = t1 - t0
            for kc in range(8):
                sq, rsq = bw(kc * 512, kc * 512 + N)
                act(sq, X.ap(kc, t0, t1), AF.Square, [X.r(kc, t0, t1)], [rsq])

        def norm_b(i):
            t0, t1 = TT[i]
            N = t1 - t0
            pb = newps()
            npb[i] = pb
            for kc in range(8):
                sq, rsq = bw(kc * 512, kc * 512 + N)
                mm(PS[pb][:, 0:N], NEGONES, sq, kc == 0, kc == 7, [rsq, r_cst], [psr(pb)], kc == 7)

        def norm_c1(i):
            t0, t1 = TT[i]
            N = t1 - t0
            pb = npb[i]
            lnv, rln = fw(0, N)
            rstd, rrs = fw(512, 512 + N)
            act(lnv, PS[pb][:, 0:N], AF.Ln, [psr(pb)], [rln], bias=EPS, scale=-1.0 / D)
            act(rstd, lnv, AF.Exp, [rln], [rrs], scale=-0.5)

        def norm_c2(i, gcol, final=False):
            t0, t1 = TT[i]
            N = t1 - t0
            rstd, rrs = fw(512, 512 + N)
            if not final:
                for kc in range(8):
                    stt(H.ap(kc, t0, t1), X.ap(kc, t0, t1), PRM[:, gcol + kc:gcol + kc + 1], rstd,
                        ALU.mult, ALU.mult, [X.r(kc, t0, t1), rrs, ('PRM', 0, NPC)], [H.r(kc, t0, t1)])
                return
            if i == 0:
                return
            for kc in range(8):
                lo = 4096 + (kc % 4) * 1024
                stg = BW[:, lo:lo + 2 * N].bitcast(F32)
                rst = ('BW', lo, lo + 1024)
                stt(stg, X.ap(kc, t0, t1), PRM[:, PC_FN + kc:PC_FN + kc + 1], rstd, ALU.mult, ALU.mult,
                    [X.r(kc, t0, t1), rrs, ('PRM', 0, NPC)], [rst])
                dst = y_d[kc * 128:(kc + 1) * 128, t0 - 16:t1 - 16]
                P.dma('sp', lambda e, dst=dst, stg=stg: e.dma_start(out=dst, in_=stg), 'out%d' % (kc % 4), [rst],
                      [('out', 0, CELL)])

        load_pmx(0)
        load_in_slab(0, 0, 0)
        for l in range(depth):
            load_in_slab(l, 1536, 2)
            if l == 0:
                for i in range(5):
                    norm_to_h(i, PC_N1 + l * 8)
                dump('h', H, 8, NT)
            ft = 1 if l == depth - 1 else 0
            ecnt = [0]
            if ft == 0:
                proj_fm(l, 0, 0, 'q')
            proj_fm(l, 1, 0, 'q')
            for i in range(5):
                proj_u(l, i, 2, ecnt)
                if i + 2 < 5:
                    proj_fm(l, i + 2, 0, 'q')
                if i == 2:
                    load_in_slab(l, 512, 0)
                if i >= ft:
                    pool_band(l, i)
            load_in_slab(l, 1024, 2)
            for i in range(2):
                proj_fm(l, i, 0, 'k')
            for i in range(2):
                proj_v(l, i, 2)
            work = {}
            for i in range(1, 4):
                work[i] = proj_fm_items(l, i + 1, 0, 'k', eng='dve') + proj_v_items(l, i + 1, 2)
            if l == 0:
                dump('q', Q, 4, NT)
            attention(work, after_work=lambda l=l: load_merge_slab(l, 0), ft=ft)
            if l == 0:
                dump('s', Q, 4, NT)
                dump('k', K, 4, NT)
                dump('v', V, 17, 512)
                dump('a', A, 4, NT)
            sgcnt = [0]
            for j in range(4):
                if j + 1 < 4:
                    load_merge_slab(l, j + 1)
                merge_slab(l, j, sgcnt, ft)
                if j == 2:
                    load_out_slab(l, 0)
            if l == 0:
                dump('mg', MG, 8, NT)
            load_out_slab(l, 1)
            load_ffn_slab(l, 0)
            out_proj(l, (PC_N2 + l * 8, False), ft)
            if l == 0:
                dump('x1', X, 8, NT)
            acnt = [0]
            for j in range(6):
                if j + 1 < 6:
                    load_ffn_slab(l, j + 1)
                elif l + 1 < depth:
                    load_pmx(l + 1)
                    load_in_slab(l + 1, 0, 0)
                if j < 5:
                    ffn_slab(l, j, acnt, ft=ft)
                elif l + 1 < depth:
                    ffn_slab(l, j, acnt, post=(PC_N1 + (l + 1) * 8, False))
                else:
                    ffn_slab(l, j, acnt, post=(PC_FN, True), ft=ft)
            if l == 0:
                dump('x2', X, 8, NT)
        P.op('sp', None, reads=[('out', 0, CELL), ('dbg', 0, CELL)])
        P.emit()
    return nc


def host_consts():
    c = np.zeros((128, NCST), np.float32)
    j = np.arange(128)[:, None]
    s = np.arange(128)[None, :]
    c[:, 0:128] = np.eye(128, dtype=np.float32)
    c[:, 128:256] = -1.0 * (j >= s)
    c[:, 256:384] = np.where(j >= s, NEG, 0.0)
    t = np.arange(128)[:, None]
    tp = np.arange(128)[None, :]
    for g in range(4):
        w = 2 << g
        gb = 384 + g * 400
        d = tp - t
        c[:, gb:gb + 128] = np.where((d >= 0) & (d <= w - 1), 1.0 / w, 0.0) - (d == 0)
        c[:, gb + 128:gb + 256] = np.where((128 + tp) - t <= w - 1, 1.0 / w, 0.0)
        t16 = np.arange(16)[:, None]
        tp16 = np.arange(16)[None, :]
        d16 = tp16 - t16
        cnt = np.minimum(tp16 + 1, w).astype(np.float32)
        c[0:16, gb + 256:gb + 272] = np.where((d16 >= 0) & (d16 <= w - 1), 1.0 / cnt, 0.0) - (d16 == 0)
        c[0:16, gb + 272:gb + 400] = np.where((16 + tp) - t16 <= w - 1, 1.0 / w, 0.0)
    return c


def host_params(norm1_g, norm2_g, final_norm_g, b_gate, pool_scale):
    p = np.zeros((128, NPC), np.float32)
    for l in range(DEPTH):
        p[:, PC_N1 + l * 8:PC_N1 + l * 8 + 8] = np.asarray(norm1_g[l]).reshape(8, 128).T
        p[:, PC_N2 + l * 8:PC_N2 + l * 8 + 8] = np.asarray(norm2_g[l]).reshape(8, 128).T
        p[:, PC_BG + l * 16:PC_BG + l * 16 + 16] = np.asarray(b_gate[l]).reshape(16, 128).T
        p[:, PC_PS + l * 4:PC_PS + l * 4 + 4] = np.asarray(pool_scale[l]).reshape(4, 128).T
    p[:, PC_FN:PC_FN + 8] = np.asarray(final_norm_g).reshape(8, 128).T
    return p


def make_in_maps(x, meta_tokens, norm1_g, w_in, b_gate, pool_mix, pool_scale, w_branch_pool,
                 w_branch_sb, w_out, norm2_g, w_ffn_in, w_ffn_out, final_norm_g):
    f = lambda a: np.ascontiguousarray(np.asarray(a, dtype=np.float32))
    x = f(x)
    shared = dict(
        metaT=f(np.asarray(meta_tokens, np.float32).T),
        params=host_params(norm1_g, norm2_g, final_norm_g, b_gate, pool_scale),
        consts=host_consts(),
        w_in=f(w_in), pool_mix=f(pool_mix), w_branch_pool=f(w_branch_pool), w_branch_sb=f(w_branch_sb),
        w_out=f(w_out), w_ffn_in=f(w_ffn_in), w_ffn_out=f(w_ffn_out),
    )
    maps = []
    for b in range(x.shape[0]):
        m = dict(shared)
        m["xT"] = f(x[b].T)
        maps.append(m)
    return maps


_NC_CACHE = {}


def kernel(x, meta_tokens, norm1_g, w_in, b_gate, pool_mix, pool_scale, w_branch_pool,
           w_branch_sb, w_out, norm2_g, w_ffn_in, w_ffn_out, final_norm_g):
    in_maps = make_in_maps(x, meta_tokens, norm1_g, w_in, b_gate, pool_mix, pool_scale, w_branch_pool,
                           w_branch_sb, w_out, norm2_g, w_ffn_in, w_ffn_out, final_norm_g)
    nc = build()
    res = run_bass_kernel_spmd(nc, in_maps, core_ids=list(range(8)))
    out = np.stack([np.ascontiguousarray(r["yT"].T) for r in res.results], axis=0)
    return out.astype(np.float32)
```

```python
import contextlib
import numpy as np
import concourse.bass as bass
import concourse.mybir as mybir
from concourse.bass_utils import run_bass_kernel_spmd

F32 = mybir.dt.float32
BF16 = mybir.dt.bfloat16
AF = mybir.ActivationFunctionType
ALU = mybir.AluOpType

D = 1024
SEQ = 2048
NMETA = 16
NT = SEQ + NMETA
DFF = 2816
DEPTH = 2
EPS = 1e-6
TT = [(0, 16)] + [(16 + 512 * i, 16 + 512 * (i + 1)) for i in range(4)]
KB = [(0, 16)] + [(16 + 128 * j, 144 + 128 * j) for j in range(16)]
CELL = 16
NEG = -30000.0
GATT = 5
FLAGS = dict(av_tp00=True, mask_after=True, pair_act=True)

PC_N1 = 0
PC_N2 = 16
PC_FN = 32
PC_BG = 40
PC_PS = 72
NPC = 80
NBAND = 1600
NCST = 384 + NBAND


class Prog:
    def __init__(self, nc):
        self.nc = nc
        self.ops = []
        self.cnt = {}
        self.arenas = {}

    def arena(self, name, nelem):
        self.arenas[name] = dict(n=(nelem + CELL - 1) // CELL, W={}, R={})

    def _record(self, eng, fn, reads, writes, stream, val, skip):
        need = {}
        for (an, lo, hi) in reads:
            A = self.arenas[an]
            c0, c1 = lo // CELL, (hi + CELL - 1) // CELL
            for s, arr in A['W'].items():
                v = int(arr[c0:c1].max())
                if v > need.get(s, 0):
                    need[s] = v
        for (an, lo, hi) in writes:
            A = self.arenas[an]
            c0, c1 = lo // CELL, (hi + CELL - 1) // CELL
            for tab in (A['W'], A['R']):
                for s, arr in tab.items():
                    v = int(arr[c0:c1].max())
                    if v > need.get(s, 0):
                        need[s] = v
        for s in skip:
            need.pop(s, None)
        for s, v in need.items():
            if s.startswith('E:'):
                assert v <= self.cnt.get(s, 0), ("dependency on a not-yet-issued inc", s, v)
        for (an, lo, hi) in reads:
            A = self.arenas[an]
            c0, c1 = lo // CELL, (hi + CELL - 1) // CELL
            if stream not in A['R']:
                A['R'][stream] = np.zeros(A['n'], np.int64)
            A['R'][stream][c0:c1] = val
        for (an, lo, hi) in writes:
            A = self.arenas[an]
            c0, c1 = lo // CELL, (hi + CELL - 1) // CELL
            if stream not in A['W']:
                A['W'][stream] = np.zeros(A['n'], np.int64)
            A['W'][stream][c0:c1] = val
        return need

    def op(self, eng, fn, reads=(), writes=(), inc=True):
        stream = 'E:' + eng
        val = self.cnt.get(stream, 0) + 1
        skip = (stream,) if eng == 'pe' else ()
        need = self._record(eng, fn, reads, writes, stream, val, skip)
        if inc:
            self.cnt[stream] = val
        self.ops.append(dict(eng=eng, fn=fn, need=need, inc=inc, dma=None))

    def dma(self, queue, fn, sem, reads=(), writes=()):
        stream = 'D:' + sem
        val = self.cnt.get(stream, 0) + 16
        need = self._record(queue, fn, reads, writes, stream, val, (stream,))
        self.cnt[stream] = val
        self.ops.append(dict(eng=queue, fn=fn, need=need, inc=True, dma=sem))

    def emit(self):
        nc = self.nc
        ops = self.ops
        sem_names = [s for s in self.cnt if self.cnt[s] > 0]
        with contextlib.ExitStack() as st:
            sems = {}
            for n in sem_names:
                sems[n] = st.enter_context(nc.semaphore(n.replace(':', '_')))
            block = st.enter_context(nc.Block())

            def run(ename, eng):
                waited = {}
                for o in ops:
                    if o['eng'] != ename:
                        continue
                    for s, v in o['need'].items():
                        if waited.get(s, 0) < v:
                            eng.wait_ge(sems[s], v)
                            waited[s] = v
                    if o['fn'] is None:
                        continue
                    ins = o['fn'](eng)
                    if o['dma'] is not None:
                        ins.then_inc(sems['D:' + o['dma']], 16)
                    elif o['inc']:
                        ins.then_inc(sems['E:' + ename], 1)

            used = set(o['eng'] for o in ops)
            if 'pe' in used:
                @block.tensor
                def _(e):
                    run('pe', e)
            if 'act' in used:
                @block.scalar
                def _(e):
                    run('act', e)
            if 'dve' in used:
                @block.vector
                def _(e):
                    run('dve', e)
            if 'pool' in used:
                @block.gpsimd
                def _(e):
                    run('pool', e)
            if 'sp' in used:
                @block.sync
                def _(e):
                    run('sp', e)


class View:
    def __init__(self, tens, name, off, A, T):
        self.tens, self.name, self.off, self.A, self.T = tens, name, off, A, T

    def ap(self, a, t0, t1, p0=0, p1=128):
        b = self.off + a * self.T
        return self.tens[p0:p1, b + t0:b + t1]

    def r(self, a, t0, t1):
        b = self.off + a * self.T
        return (self.name, b + t0, b + t1)


def build(depth=DEPTH, dbg=None):
    nc = bass.Bass("TRN2", target_bir_lowering=False)
    dt = nc.dram_tensor
    xT_d = dt("xT", [D, SEQ], F32, kind="ExternalInput").ap()
    meta_d = dt("metaT", [D, NMETA], F32, kind="ExternalInput").ap()
    prm_d = dt("params", [128, NPC], F32, kind="ExternalInput").ap()
    cst_d = dt("consts", [128, NCST], F32, kind="ExternalInput").ap()
    w_in_d = dt("w_in", [DEPTH, D, 4096], F32, kind="ExternalInput").ap()
    pmix_d = dt("pool_mix", [DEPTH, 4, 128, 128], F32, kind="ExternalInput").ap()
    wbp_d = dt("w_branch_pool", [DEPTH, 512, D], F32, kind="ExternalInput").ap()
    wbs_d = dt("w_branch_sb", [DEPTH, 512, D], F32, kind="ExternalInput").ap()
    wout_d = dt("w_out", [DEPTH, D, D], F32, kind="ExternalInput").ap()
    wfi_d = dt("w_ffn_in", [DEPTH, D, 2 * DFF], F32, kind="ExternalInput").ap()
    wfo_d = dt("w_ffn_out", [DEPTH, DFF, D], F32, kind="ExternalInput").ap()
    y_d = dt("yT", [D, SEQ], F32, kind="ExternalOutput").ap()
    dbg_d = {}
    if dbg:
        for name, shape in dbg.items():
            dbg_d[name] = dt("dbg_" + name, shape, F32, kind="ExternalOutput").ap()

    P = Prog(nc)
    with contextlib.ExitStack() as st:
        def sb(name, n, dtype):
            P.arena(name, n)
            return st.enter_context(nc.sbuf_tensor(name, [128, n], dtype))

        Xt = sb("X", 8 * NT, F32)
        PRM = sb("PRM", NPC, F32)
        FW = sb("FW", 2048, F32)
        Ht = sb("H", 8 * NT, BF16)
        MIX = sb("MIX", 33472, BF16)
        WB = sb("WB", 4 * 2048, BF16)
        CST = sb("CST", 512 + NBAND, BF16)
        BW = sb("BW", 8192, BF16)
        ZER = sb("ZER", 128, BF16)
        PMX = sb("PMX", 512, BF16)
        PSB = [st.enter_context(nc.psum_tensor("psb%d" % b, [128, 1024], F32)) for b in range(4)]

        class _Bank:
            def __init__(self, b):
                self.t, self.o = PSB[b // 2], (b % 2) * 512

            def __getitem__(self, idx):
                ps_, cs_ = idx
                c0 = cs_.start or 0
                c1 = 512 if cs_.stop is None else cs_.stop
                return self.t[ps_, self.o + c0:self.o + c1]

        PS = [_Bank(b) for b in range(8)]
        P.arena('ps', 8 * CELL)
        P.arena('out', CELL)
        P.arena('dbg', CELL)

        X = View(Xt, "X", 0, 8, NT)
        H = View(Ht, "H", 0, 8, NT)
        Q = View(MIX, "MIX", 0, 4, NT)
        K = View(MIX, "MIX", 8256, 4, NT)
        V = View(MIX, "MIX", 16512, 17, 512)
        A = View(MIX, "MIX", 25216, 4, NT)
        MG = View(MIX, "MIX", 8256, 8, NT)

        def psr(b):
            return ('ps', b * CELL, (b + 1) * CELL)

        def fw(lo, hi):
            return FW[:, lo:hi], ('FW', lo, hi)

        def bw(lo, hi):
            return BW[:, lo:hi], ('BW', lo, hi)

        IDENT = CST[:, 0:128]
        NEGU = CST[:, 128:256]
        NEGM = CST[:, 256:384]
        NEGONES = CST[:, 384:512]
        r_cst = ('CST', 0, 512)

        psn = [0]
        ps_pool = [list(range(8))]

        def newps():
            pool_ = ps_pool[0]
            b = pool_[psn[0] % len(pool_)]
            psn[0] += 1
            return b

        def mm(out, lhsT, rhs, start, stop, reads, writes, inc, tp=None):
            if tp is None:
                P.op('pe', lambda e: e.matmul(out, lhsT=lhsT, rhs=rhs, start=start, stop=stop),
                     reads, writes, inc)
            else:
                P.op('pe', lambda e: e.matmul(out, lhsT=lhsT, rhs=rhs, start=start, stop=stop,
                                              tile_position=tp), reads, writes, inc)

        def mmx(out, lhsT, rhs, start, stop, reads, writes, inc):
            P.op('pe', lambda e: e.matmul(out, lhsT=lhsT, rhs=rhs, start=start, stop=stop,
                                          skip_group_check=True), reads, writes, inc)

        def act(out, in_, func, reads, writes, bias=None, scale=None):
            kw = {}
            if bias is not None:
                kw['bias'] = bias
            if scale is not None:
                kw['scale'] = scale
            P.op('act', lambda e: e.activation(out=out, in_=in_, func=func, **kw), reads, writes)

        def tt(eng, out, in0, in1, op, reads, writes):
            P.op(eng, lambda e: e.tensor_tensor(out=out, in0=in0, in1=in1, op=op), reads, writes)

        def stt(out, in0, scalar, in1, op0, op1, reads, writes):
            P.op('dve', lambda e: e.scalar_tensor_tensor(out=out, in0=in0, scalar=scalar, in1=in1,
                                                         op0=op0, op1=op1), reads, writes)

        def cp(eng, out, in_, reads, writes):
            if eng == 'act':
                P.op(eng, lambda e: e.copy(out=out, in_=in_), reads, writes)
            else:
                P.op(eng, lambda e: e.tensor_copy(out=out, in_=in_), reads, writes)

        def memset(eng, ap, val, writes):
            P.op(eng, lambda e: e.memset(ap, val), (), writes)

        def wdma(out, in_, sem, writes):
            P.dma('pool', lambda e: e.dma_start(out=out, in_=in_), sem, (), writes)

        P.dma('sp', lambda e: e.dma_start(out=PRM[:, :], in_=prm_d), 'prm', (), [('PRM', 0, NPC)])
        wdma(CST[:, 0:384], cst_d[:, 0:384], 'cst', [('CST', 0, 384)])
        wdma(CST[:, 512:512 + NBAND], cst_d[:, 384:NCST], 'cst2', [('CST', 512, 512 + NBAND)])
        memset('dve', CST[:, 384:512], -1.0, [('CST', 384, 512)])
        memset('dve', ZER[:, :], 0.0, [('ZER', 0, 128)])
        X3 = Xt[:, :].rearrange("p (k t) -> p k t", t=NT)
        P.dma('sp', lambda e: e.dma_start(out=X3[:, :, 0:16], in_=meta_d.rearrange("(k p) t -> p k t", p=128)),
              'x0', (), [X.r(kc, 0, 16) for kc in range(8)])
        for i in range(1, 5):
            t0, t1 = TT[i]
            src = xT_d[:, t0 - 16:t1 - 16].rearrange("(k p) t -> p k t", p=128)
            dst = X3[:, :, t0:t1]
            P.dma('sp', lambda e, dst=dst, src=src: e.dma_start(out=dst, in_=src),
                  'x%d' % i, (), [X.r(kc, t0, t1) for kc in range(8)])

        def wunit(u):
            if u < 4:
                return WB, 'WB', u * 2048
            return BW, 'BW', (u - 4) * 2048

        def load_in_slab(l, col0, u0):
            off = u0 * 2048
            dst = WB[:, off:off + 4096].rearrange("p (k n) -> p k n", n=512)
            src = w_in_d[l, :, col0:col0 + 512].rearrange("(k p) n -> p k n", p=128)
            wdma(dst, src, 'wu%d' % u0, [('WB', off, off + 4096)])

        def load_merge_slab(l, j):
            us = (0, 1, 2) if j % 2 == 0 else (3, 4, 5)
            for idx, col0 in ((0, 2048 + 256 * j), (1, 3072 + 256 * j)):
                T_, an, off = wunit(us[idx])
                dst = T_[:, off:off + 2048].rearrange("p (k n) -> p k n", n=256)
                src = w_in_d[l, :, col0:col0 + 256].rearrange("(k p) n -> p k n", p=128)
                wdma(dst, src, 'wu%d' % us[idx], [(an, off, off + 2048)])
            T_, an, off = wunit(us[2])
            for idx, wd in ((0, wbp_d), (1, wbs_d)):
                dst = T_[:, off + idx * 1024:off + idx * 1024 + 1024].rearrange("p (k n) -> p k n", n=256)
                src = wd[l, :, 256 * j:256 * j + 256].rearrange("(k p) n -> p k n", p=128)
                wdma(dst, src, 'wu%d' % us[2], [(an, off, off + 2048)])

        def load_out_slab(l, jj):
            off = jj * 4096
            dst = WB[:, off:off + 4096].rearrange("p (k n) -> p k n", n=512)
            src = wout_d[l, :, 512 * jj:512 * jj + 512].rearrange("(k p) n -> p k n", p=128)
            wdma(dst, src, 'wu%d' % (2 * jj), [('WB', off, off + 4096)])

        FFN_SLABS = [(0, 2), (2, 4), (6, 4), (10, 4), (14, 4), (18, 4)]

        def ffn_nfc(j):
            return FFN_SLABS[j][1]

        FFN_SLOT = [(0, 25216), (8256, 16448)]
        FFN_ACT = 29312

        def ffn_ranges(j):
            wib, wob = FFN_SLOT[j % 2]
            return [('MIX', wib, wib + 8192), ('MIX', wob, wob + 4096)]

        def load_ffn_slab(l, j):
            wib, wob = FFN_SLOT[j % 2]
            nfc = ffn_nfc(j)
            nc_ = 128 * nfc
            wi4 = MIX[:, wib:wib + 8192].rearrange("p (k g c) -> p k g c", k=8, g=2)
            sem = 'f%d' % (j % 2)
            for gu in range(2):
                fo = FFN_SLABS[j][0] * 128
                src = wfi_d[l, :, gu * DFF + fo:gu * DFF + fo + nc_].rearrange("(k p) n -> p k n", p=128)
                wdma(wi4[:, :, gu, 0:nc_], src, sem, ffn_ranges(j))
            dst = MIX[:, wob:wob + nfc * 1024].rearrange("p (f n) -> p f n", n=1024)
            fo = FFN_SLABS[j][0] * 128
            src = wfo_d[l, fo:fo + nc_, :].rearrange("(f p) n -> p f n", p=128)
            wdma(dst, src, sem, ffn_ranges(j))

        def load_pmx(l):
            dst = PMX[:, :].rearrange("p (g d) -> p g d", d=128)
            src = pmix_d[l].rearrange("g c d -> c g d")
            wdma(dst, src, 'pmx', [('PMX', 0, 512)])

        def norm_stats(i):
            t0, t1 = TT[i]
            N = t1 - t0
            pb = newps()
            for kc in range(8):
                sq, rsq = bw((kc % 2) * 512, (kc % 2) * 512 + N)
                act(sq, X.ap(kc, t0, t1), AF.Square, [X.r(kc, t0, t1)], [rsq])
                mm(PS[pb][:, 0:N], NEGONES, sq, kc == 0, kc == 7, [rsq, r_cst], [psr(pb)], True)
            lnv, rln = fw(0, N)
            rstd, rrs = fw(512, 512 + N)
            act(lnv, PS[pb][:, 0:N], AF.Ln, [psr(pb)], [rln], bias=EPS, scale=-1.0 / D)
            act(rstd, lnv, AF.Exp, [rln], [rrs], scale=-0.5)
            return rstd, rrs

        def norm_to_h(i, gcol):
            t0, t1 = TT[i]
            rstd, rrs = norm_stats(i)
            for kc in range(8):
                stt(H.ap(kc, t0, t1), X.ap(kc, t0, t1), PRM[:, gcol + kc:gcol + kc + 1], rstd,
                    ALU.mult, ALU.mult, [X.r(kc, t0, t1), rrs, ('PRM', 0, NPC)], [H.r(kc, t0, t1)])

        def tile_kbs(i):
            return [0] if i == 0 else list(range(4 * (i - 1) + 1, 4 * i + 1))

        r_band = ('CST', 512, 512 + NBAND)

        def proj_u(l, i, u0, ecnt):
            off = u0 * 2048
            rw = ('WB', off, off + 4096)
            for kb in tile_kbs(i):
                b0, b1 = KB[kb]
                nk = b1 - b0
                pb = newps()
                for kc in range(8):
                    mm(PS[pb][0:nk, 0:512], H.ap(kc, b0, b1), WB[:, off + kc * 512:off + kc * 512 + 512],
                       kc == 0, kc == 7, [rw, H.r(kc, b0, b1)], [psr(pb)], kc == 7)
                ub = 2048 + (kb % 6) * 512
                eng = 'act' if ecnt[0] % 2 == 0 else 'dve'
                ecnt[0] += 1
                cp(eng, BW[0:nk, ub:ub + 512], PS[pb][0:nk, 0:512], [psr(pb)], [('BW', ub, ub + 512)])

        def pool_band(l, i):
            t0, t1 = TT[i]
            N = t1 - t0
            kbs = tile_kbs(i)
            pend = []

            def band(g):
                pd = newps()
                gb = 512 + g * 400
                for j, kb in enumerate(kbs):
                    b0, b1 = KB[kb]
                    nk = b1 - b0
                    ub = 2048 + (kb % 6) * 512
                    c0 = j * 128
                    if kb == 0:
                        mmx(PS[pd][:, c0:c0 + nk], BW[0:16, ub + g * 128:ub + g * 128 + 128],
                            CST[0:16, gb + 256:gb + 272], True, True, [('BW', ub, ub + 512), r_band], [psr(pd)],
                            j == len(kbs) - 1)
                        continue
                    mmx(PS[pd][:, c0:c0 + nk], BW[:, ub + g * 128:ub + g * 128 + 128], CST[:, gb:gb + 128],
                        True, False, [('BW', ub, ub + 512), r_band], [psr(pd)], False)
                    pk = kb - 1
                    pub = 2048 + (pk % 6) * 512
                    if pk == 0:
                        mmx(PS[pd][:, c0:c0 + nk], BW[0:16, pub + g * 128:pub + g * 128 + 128],
                            CST[0:16, gb + 272:gb + 400], False, True, [('BW', pub, pub + 512), r_band],
                            [psr(pd)], j == len(kbs) - 1)
                    else:
                        mmx(PS[pd][:, c0:c0 + nk], BW[:, pub + g * 128:pub + g * 128 + 128],
                            CST[:, gb + 128:gb + 256], False, True, [('BW', pub, pub + 512), r_band],
                            [psr(pd)], j == len(kbs) - 1)
                db = 1024 + (g % 2) * 512
                cp('dve', BW[:, db:db + N], PS[pd][:, 0:N], [psr(pd)], [('BW', db, db + 512)])
                return db

            def mix(g, db):
                pb2 = newps()
                mm(PS[pb2][:, 0:N], PMX[:, g * 128:(g + 1) * 128], BW[:, db:db + N], True, True,
                   [('BW', db, db + 512), ('PMX', 0, 512)], [psr(pb2)], True)
                pc = PC_PS + l * 4 + g
                act(A.ap(g, t0, t1), PS[pb2][:, 0:N], AF.Identity, [psr(pb2), ('PRM', 0, NPC)], [A.r(g, t0, t1)],
                    scale=PRM[:, pc:pc + 1])

            d0 = band(0)
            d1 = band(1)
            mix(0, d0)
            d2 = band(2)
            mix(1, d1)
            d3 = band(3)
            mix(2, d2)
            mix(3, d3)

        def proj_fm_items(l, i, u0, kind, eng='act'):
            t0, t1 = TT[i]
            N = t1 - t0
            off = u0 * 2048
            rw = ('WB', off, off + 4096)
            items = []
            for n in range(4):
                def item(n=n):
                    pb = newps()
                    for kc in range(8):
                        mm(PS[pb][:, 0:N], WB[:, off + kc * 512 + n * 128:off + kc * 512 + n * 128 + 128],
                           H.ap(kc, t0, t1), kc == 0, kc == 7, [rw, H.r(kc, t0, t1)], [psr(pb)], kc == 7)
                    if kind == 'k':
                        cp(eng, K.ap(n, t0, t1), PS[pb][:, 0:N], [psr(pb)], [K.r(n, t0, t1)])
                    else:
                        act(Q.ap(n, t0, t1), PS[pb][:, 0:N], AF.Identity, [psr(pb)], [Q.r(n, t0, t1)], scale=0.125)
                items.append(item)
            return items

        def proj_fm(l, i, u0, kind):
            for it in proj_fm_items(l, i, u0, kind):
                it()

        def proj_v_items(l, i, u0):
            off = u0 * 2048
            rw = ('WB', off, off + 4096)
            items = []
            for kb in tile_kbs(i):
                def item(kb=kb):
                    b0, b1 = KB[kb]
                    nk = b1 - b0
                    pb = newps()
                    for kc in range(8):
                        mm(PS[pb][0:nk, 0:512], H.ap(kc, b0, b1), WB[:, off + kc * 512:off + kc * 512 + 512],
                           kc == 0, kc == 7, [rw, H.r(kc, b0, b1)], [psr(pb)], kc == 7)
                    cp('dve', V.ap(kb, 0, 512, 0, nk), PS[pb][0:nk, 0:512], [psr(pb)], [V.r(kb, 0, 512)])
                items.append(item)
            return items

        def proj_v(l, i, u0):
            for it in proj_v_items(l, i, u0):
                it()

        def attention(work=None, after_work=None, ft=0):
            PO = 6
            ps_pool[0] = [7]
            SB_S = GATT * 1024
            assert GATT <= 5
            zc = [0]
            wc = [0]
            for i in range(ft, 5):
                t0, t1 = TT[i]
                N = t1 - t0
                kbs = [0] if i == 0 else list(range(4 * i, -1, -1))
                first_diag = 0 if i == 0 else 4 * (i - 1) + 1
                n = len(kbs)

                def geom(si):
                    kb = kbs[si]
                    b0, b1 = KB[kb]
                    diag = kb >= first_diag
                    c0 = (kb - first_diag) * 128 if (diag and i > 0) else 0
                    return kb, b0, b1, b1 - b0, diag, c0

                def pair(T_, base, nk, c0):
                    return T_[0:nk, base:base + 1024].rearrange("p (h c) -> p h c", h=2)[:, :, c0:N]

                def qk(hp, si, r):
                    kb, b0, b1, nk, diag, c0 = geom(si)
                    dw = min(128, N - c0)
                    for hh in range(2):
                        pz = 2 * r + hh
                        p0, p1 = 64 * hh, 64 * hh + 64
                        mmx(PS[pz][0:nk, c0:N], K.ap(hp, b0, b1, p0, p1), Q.ap(hp, t0 + c0, t1, p0, p1),
                            True, not diag, [K.r(hp, b0, b1), Q.r(hp, t0 + c0, t1)], [psr(pz)], not diag)
                    if diag:
                        for hh in range(2):
                            pz = 2 * r + hh
                            mmx(PS[pz][0:nk, c0:c0 + dw], CST[0:nk, 0:nk], CST[0:nk, 256:256 + dw],
                                False, True, [r_cst], [psr(pz)], True)

                def prep(a):
                    kind, hp, si, gi = a['kind'], a['hp'], a['si'], a['gi']
                    kb, b0, b1, nk, diag, c0 = geom(si)
                    r = zc[0] % 3
                    zc[0] += 1
                    a['r'] = r
                    qk(hp, si, r)
                    if kind == 'ex':
                        sp_r = ('BW', gi * 1024, gi * 1024 + 1024)
                        so = SB_S + (si % 2) * 1024
                        sn = SB_S + ((si + 1) % 2) * 1024
                        s_r = ('BW', so, so + 1024)
                        if si == 0 and n > 1:
                            memset('dve', BW[:, SB_S:SB_S + 2048], 0.0, [('BW', SB_S, SB_S + 2048)])
                        for hh in range(2):
                            pz = 2 * r + hh
                            sp_ap = BW[0:nk, gi * 1024 + hh * 512 + c0:gi * 1024 + hh * 512 + N]
                            mmx(PS[pz][0:nk, c0:N], CST[0:nk, 128:128 + nk], sp_ap, False, si == 0,
                                [r_cst, sp_r], [psr(pz)], si == 0)
                            if si > 0:
                                mmx(PS[pz][0:nk, c0:N], CST[:, 384:384 + nk],
                                    BW[:, so + hh * 512 + c0:so + hh * 512 + N], False, True,
                                    [r_cst, s_r], [psr(pz)], True)
                        if si < n - 1:
                            tt('dve', pair(BW, sn, 128, c0), pair(BW, so, 128, c0), pair(BW, gi * 1024, 128, c0),
                               ALU.add, [s_r, sp_r], [('BW', sn, sn + 1024)])

                def wview(gi, nk):
                    if gi < 4:
                        return FW[0:nk, gi * 512:(gi + 1) * 512].bitcast(BF16), ('FW', gi * 512, gi * 512 + 512)
                    return BW[0:nk, 7168:8192], ('BW', 7168, 8192)

                def actop(a):
                    kind, hp, si, gi, r = a['kind'], a['hp'], a['si'], a['gi'], a['r']
                    kb, b0, b1, nk, diag, c0 = geom(si)
                    if kind == 'sp':
                        act(pair(BW, gi * 1024, nk, c0), pair(PSB[r], 0, nk, c0), AF.Softplus,
                            [psr(2 * r), psr(2 * r + 1)], [('BW', gi * 1024, gi * 1024 + 1024)])
                    else:
                        wv, w_r = wview(gi, nk)
                        act(wv.rearrange("p (h c) -> p h c", h=2)[:, :, c0:N], pair(PSB[r], 0, nk, c0), AF.Exp,
                            [psr(2 * r), psr(2 * r + 1)], [w_r])

                pending = []

                def post(a):
                    if a['kind'] != 'ex':
                        return
                    hp, si, gi = a['hp'], a['si'], a['gi']

                    def emit_av():
                        kb, b0, b1, nk, diag, c0 = geom(si)
                        wv, w_r = wview(gi, nk)
                        if si == 0:
                            mm(PS[PO][:, 0:N], ZER[:, 0:128], CST[:, 0:N], True, False, [('ZER', 0, 128), r_cst],
                               [psr(PO)], False)
                        last = (si == n - 1)
                        for hh in range(2):
                            h = 2 * hp + hh
                            w_ap = wv[:, hh * 512 + c0:hh * 512 + N]
                            mm(PS[PO][64 * hh:64 * hh + 64, c0:N], V.ap(kb, h * 64, h * 64 + 64, 0, nk), w_ap,
                               False, last, [V.r(kb, 0, 512), w_r], [psr(PO)], hh == 1, tp=(0, 64 * hh))
                        if last:
                            cp('dve', Q.ap(hp, t0, t1), PS[PO][:, 0:N], [psr(PO)], [Q.r(hp, t0, t1)])
                    pending.append(emit_av)

                entries = [(hp, si) for hp in range(4) for si in range(n)]
                acts = []
                for j in range(0, len(entries), GATT):
                    grp = entries[j:j + GATT]
                    for gi, (hp, si) in enumerate(grp):
                        acts.append(dict(kind='sp', hp=hp, si=si, gi=gi))
                    for gi, (hp, si) in enumerate(grp):
                        acts.append(dict(kind='ex', hp=hp, si=si, gi=gi))
                for k in range(min(2, len(acts))):
                    prep(acts[k])
                for k in range(len(acts)):
                    if k + 2 < len(acts):
                        prep(acts[k + 2])
                    if acts[k]['kind'] == 'ex' and acts[k]['gi'] == 0:
                        while pending:
                            pending.pop(0)()
                    actop(acts[k])
                    post(acts[k])
                    if acts[k]['kind'] == 'sp' and pending:
                        pending.pop(0)()
                    if work is not None and acts[k]['kind'] == 'sp' and acts[k]['gi'] < 2 and work.get(i):
                        work[i].pop(0)()
                while pending:
                    pending.pop(0)()
                if work is not None:
                    while work.get(i):
                        work[i].pop(0)()
                    if after_work is not None and i == max(work):
                        after_work()
            ps_pool[0] = list(range(8))

        def merge_slab(l, j, sgcnt, ft=0):
            us = (0, 1, 2) if j % 2 == 0 else (3, 4, 5)
            Tg, ang, offg = wunit(us[0])
            Ts, ans, offs = wunit(us[1])
            Tb, anb, offb = wunit(us[2])
            for i in range(ft, 5):
                t0, t1 = TT[i]
                N = t1 - t0
                for nn in range(2):
                    n = 2 * j + nn
                    pbs = [newps() for _ in range(4)]
                    for kc in range(8):
                        mm(PS[pbs[0]][:, 0:N], Tg[:, offg + kc * 256 + nn * 128:offg + kc * 256 + nn * 128 + 128],
                           H.ap(kc, t0, t1), kc == 0, kc == 7, [(ang, offg, offg + 2048), H.r(kc, t0, t1)],
                           [psr(pbs[0])], kc == 7)
                    for kc in range(8):
                        mm(PS[pbs[1]][:, 0:N], Ts[:, offs + kc * 256 + nn * 128:offs + kc * 256 + nn * 128 + 128],
                           H.ap(kc, t0, t1), kc == 0, kc == 7, [(ans, offs, offs + 2048), H.r(kc, t0, t1)],
                           [psr(pbs[1])], kc == 7)
                    for c in range(4):
                        o_ = offb + c * 256 + nn * 128
                        mm(PS[pbs[2]][:, 0:N], Tb[:, o_:o_ + 128], A.ap(c, t0, t1), c == 0, c == 3,
                           [(anb, offb, offb + 2048), A.r(c, t0, t1)], [psr(pbs[2])], c == 3)
                    for c in range(4):
                        o_ = offb + 1024 + c * 256 + nn * 128
                        mm(PS[pbs[3]][:, 0:N], Tb[:, o_:o_ + 128], Q.ap(c, t0, t1), c == 0, c == 3,
                           [(anb, offb, offb + 2048), Q.r(c, t0, t1)], [psr(pbs[3])], c == 3)
                    k3 = sgcnt[0] % 2
                    sgcnt[0] += 1
                    g1, rg1 = fw(k3 * 1024, k3 * 1024 + N)
                    g2, rg2 = fw(k3 * 1024 + 512, k3 * 1024 + 512 + N)
                    bc = PC_BG + l * 16
                    act(g1, PS[pbs[0]][:, 0:N], AF.Sigmoid, [psr(pbs[0]), ('PRM', 0, NPC)], [rg1],
                        bias=PRM[:, bc + n:bc + n + 1])
                    act(g2, PS[pbs[1]][:, 0:N], AF.Sigmoid, [psr(pbs[1]), ('PRM', 0, NPC)], [rg2],
                        bias=PRM[:, bc + 8 + n:bc + 8 + n + 1])
                    tt('dve', g1, g1, PS[pbs[2]][:, 0:N], ALU.mult, [rg1, psr(pbs[2])], [rg1])
                    tt('dve', g2, g2, PS[pbs[3]][:, 0:N], ALU.mult, [rg2, psr(pbs[3])], [rg2])
                    tt('dve', MG.ap(n, t0, t1), g1, g2, ALU.add, [rg1, rg2], [MG.r(n, t0, t1)])

        def out_proj(l, post, ft=0):
            order = [1, 0, 2, 3, 4] if ft == 0 else [1, 2, 3, 4]
            prev = None
            for i in order:
                t0, t1 = TT[i]
                N = t1 - t0
                if prev is not None:
                    norm_a(prev)
                for n in range(8):
                    jj, nn = divmod(n, 4)
                    off = jj * 4096
                    rw = ('WB', off, off + 4096)
                    pb = newps()
                    for kc in range(8):
                        mm(PS[pb][:, 0:N], WB[:, off + kc * 512 + nn * 128:off + kc * 512 + nn * 128 + 128],
                           MG.ap(kc, t0, t1), kc == 0, kc == 7, [rw, MG.r(kc, t0, t1)], [psr(pb)], kc == 7)
                    tt('dve', X.ap(n, t0, t1), X.ap(n, t0, t1), PS[pb][:, 0:N], ALU.add,
                       [X.r(n, t0, t1), psr(pb)], [X.r(n, t0, t1)])
                    if n == 3 and prev is not None:
                        norm_b(prev)
                        norm_c1(prev)
                if prev is not None:
                    norm_c2(prev, post[0], post[1])
                prev = i
            norm_a(prev)
            norm_b(prev)
            norm_c1(prev)
            norm_c2(prev, post[0], post[1])

        def ffn_slab(l, j, acnt, post=None, ft=0):
            wib, wob = FFN_SLOT[j % 2]
            nfc = ffn_nfc(j)
            rws = ffn_ranges(j)
            for i in range(ft, 5):
                t0, t1 = TT[i]
                N = t1 - t0
                par = acnt[0] % 2
                acnt[0] += 1
                ab = FFN_ACT + par * 2048
                if post is not None and i > ft:
                    norm_a(i - 1)
                for fc in range(nfc):
                    pbg, pbu = newps(), newps()
                    for gu, pb in ((0, pbg), (1, pbu)):
                        for kc in range(8):
                            o_ = wib + kc * 1024 + gu * 512 + fc * 128
                            mm(PS[pb][:, 0:N], MIX[:, o_:o_ + 128], H.ap(kc, t0, t1), kc == 0, kc == 7,
                               rws + [H.r(kc, t0, t1)], [psr(pb)], kc == 7)
                    sb_ = 1024 + (fc % 2) * 512
                    sl, rsl = fw(sb_, sb_ + N)
                    act(sl, PS[pbg][:, 0:N], AF.Silu, [psr(pbg)], [rsl])
                    tt('dve', MIX[:, ab + fc * 512:ab + fc * 512 + N], sl, PS[pbu][:, 0:N], ALU.mult,
                       [rsl, psr(pbu)], [('MIX', ab + fc * 512, ab + fc * 512 + 512)])
                if post is not None and i > ft:
                    norm_b(i - 1)
                    norm_c1(i - 1)
                for n in range(8):
                    pb = newps()
                    for fc in range(nfc):
                        o_ = wob + fc * 1024 + n * 128
                        mm(PS[pb][:, 0:N], MIX[:, o_:o_ + 128], MIX[:, ab + fc * 512:ab + fc * 512 + N],
                           fc == 0, fc == nfc - 1, rws + [('MIX', ab + fc * 512, ab + fc * 512 + 512)],
                           [psr(pb)], fc == nfc - 1)
                    tt('dve', X.ap(n, t0, t1), X.ap(n, t0, t1), PS[pb][:, 0:N], ALU.add,
                       [X.r(n, t0, t1), psr(pb)], [X.r(n, t0, t1)])
                if post is not None and i > ft:
                    norm_c2(i - 1, post[0], post[1])
            if post is not None:
                norm_a(4)
                norm_b(4)
                norm_c1(4)
                norm_c2(4, post[0], post[1])

        def dump(name, view, nchunk, tlen):
            if name not in dbg_d:
                return
            for a in range(nchunk):
                for c0 in range(0, tlen, 512):
                    c1 = min(tlen, c0 + 512)
                    stg, rst = fw(1024, 1024 + (c1 - c0))
                    cp('dve', stg, view.ap(a, c0, c1), [view.r(a, c0, c1)], [rst])
                    dst = dbg_d[name][:, a * tlen + c0:a * tlen + c1]
                    P.dma('sp', lambda e, dst=dst, stg=stg: e.dma_start(out=dst, in_=stg), 'dbg', [rst],
                          [('dbg', 0, CELL)])

        npb = {}

        def norm_a(i):
            t0, t1 = TT[i]
            N = t1 - t0
            for kc in range(8):
                sq, rsq = bw(kc * 512, kc * 512 + N)
                act(sq, X.ap(kc, t0, t1), AF.Square, [X.r(kc, t0, t1)], [rsq])

        def norm_b(i):
            t0, t1 = TT[i]
            N = t1 - t0
            pb = newps()
            npb[i] = pb
            for kc in range(8):
                sq, rsq = bw(kc * 512, kc * 512 + N)
                mm(PS[pb][:, 0:N], NEGONES, sq, kc == 0, kc == 7, [rsq, r_cst], [psr(pb)], kc == 7)

        def norm_c1(i):
            t0, t1 = TT[i]
            N = t1 - t0
            pb = npb[i]
            lnv, rln = fw(0, N)
            rstd, rrs = fw(512, 512 + N)
            act(lnv, PS[pb][:, 0:N], AF.Ln, [psr(pb)], [rln], bias=EPS, scale=-1.0 / D)
            act(rstd, lnv, AF.Exp, [rln], [rrs], scale=-0.5)

        def norm_c2(i, gcol, final=False):
            t0, t1 = TT[i]
            N = t1 - t0
            rstd, rrs = fw(512, 512 + N)
            if not final:
                for kc in range(8):
                    stt(H.ap(kc, t0, t1), X.ap(kc, t0, t1), PRM[:, gcol + kc:gcol + kc + 1], rstd,
                        ALU.mult, ALU.mult, [X.r(kc, t0, t1), rrs, ('PRM', 0, NPC)], [H.r(kc, t0, t1)])
                return
            if i == 0:
                return
            for kc in range(8):
                lo = 4096 + (kc % 4) * 1024
                stg = BW[:, lo:lo + 2 * N].bitcast(F32)
                rst = ('BW', lo, lo + 1024)
                stt(stg, X.ap(kc, t0, t1), PRM[:, PC_FN + kc:PC_FN + kc + 1], rstd, ALU.mult, ALU.mult,
                    [X.r(kc, t0, t1), rrs, ('PRM', 0, NPC)], [rst])
                dst = y_d[kc * 128:(kc + 1) * 128, t0 - 16:t1 - 16]
                P.dma('sp', lambda e, dst=dst, stg=stg: e.dma_start(out=dst, in_=stg), 'out%d' % (kc % 4), [rst],
                      [('out', 0, CELL)])

        load_pmx(0)
        load_in_slab(0, 0, 0)
        for l in range(depth):
            load_in_slab(l, 1536, 2)
            if l == 0:
                for i in range(5):
                    norm_to_h(i, PC_N1 + l * 8)
                dump('h', H, 8, NT)
            ft = 1 if l == depth - 1 else 0
            ecnt = [0]
            if ft == 0:
                proj_fm(l, 0, 0, 'q')
            proj_fm(l, 1, 0, 'q')
            for i in range(5):
                proj_u(l, i, 2, ecnt)
                if i + 2 < 5:
                    proj_fm(l, i + 2, 0, 'q')
                if i == 2:
                    load_in_slab(l, 512, 0)
                if i >= ft:
                    pool_band(l, i)
            load_in_slab(l, 1024, 2)
            for i in range(2):
                proj_fm(l, i, 0, 'k')
            for i in range(2):
                proj_v(l, i, 2)
            work = {}
            for i in range(1, 4):
                work[i] = proj_fm_items(l, i + 1, 0, 'k', eng='dve') + proj_v_items(l, i + 1, 2)
            if l == 0:
                dump('q', Q, 4, NT)
            attention(work, after_work=lambda l=l: load_merge_slab(l, 0), ft=ft)
            if l == 0:
                dump('s', Q, 4, NT)
                dump('k', K, 4, NT)
                dump('v', V, 17, 512)
                dump('a', A, 4, NT)
            sgcnt = [0]
            for j in range(4):
                if j + 1 < 4:
                    load_merge_slab(l, j + 1)
                merge_slab(l, j, sgcnt, ft)
                if j == 2:
                    load_out_slab(l, 0)
            if l == 0:
                dump('mg', MG, 8, NT)
            load_out_slab(l, 1)
            load_ffn_slab(l, 0)
            out_proj(l, (PC_N2 + l * 8, False), ft)
            if l == 0:
                dump('x1', X, 8, NT)
            acnt = [0]
            for j in range(6):
                if j + 1 < 6:
                    load_ffn_slab(l, j + 1)
                elif l + 1 < depth:
                    load_pmx(l + 1)
                    load_in_slab(l + 1, 0, 0)
                if j < 5:
                    ffn_slab(l, j, acnt, ft=ft)
                elif l + 1 < depth:
                    ffn_slab(l, j, acnt, post=(PC_N1 + (l + 1) * 8, False))
                else:
                    ffn_slab(l, j, acnt, post=(PC_FN, True), ft=ft)
            if l == 0:
                dump('x2', X, 8, NT)
        P.op('sp', None, reads=[('out', 0, CELL), ('dbg', 0, CELL)])
        P.emit()
    return nc


def host_consts():
    c = np.zeros((128, NCST), np.float32)
    j = np.arange(128)[:, None]
    s = np.arange(128)[None, :]
    c[:, 0:128] = np.eye(128, dtype=np.float32)
    c[:, 128:256] = -1.0 * (j >= s)
    c[:, 256:384] = np.where(j >= s, NEG, 0.0)
    t = np.arange(128)[:, None]
    tp = np.arange(128)[None, :]
    for g in range(4):
        w = 2 << g
        gb = 384 + g * 400
        d = tp - t
        c[:, gb:gb + 128] = np.where((d >= 0) & (d <= w - 1), 1.0 / w, 0.0) - (d == 0)
        c[:, gb + 128:gb + 256] = np.where((128 + tp) - t <= w - 1, 1.0 / w, 0.0)
        t16 = np.arange(16)[:, None]
        tp16 = np.arange(16)[None, :]
        d16 = tp16 - t16
        cnt = np.minimum(tp16 + 1, w).astype(np.float32)
        c[0:16, gb + 256:gb + 272] = np.where((d16 >= 0) & (d16 <= w - 1), 1.0 / cnt, 0.0) - (d16 == 0)
        c[0:16, gb + 272:gb + 400] = np.where((16 + tp) - t16 <= w - 1, 1.0 / w, 0.0)
    return c


def host_params(norm1_g, norm2_g, final_norm_g, b_gate, pool_scale):
    p = np.zeros((128, NPC), np.float32)
    for l in range(DEPTH):
        p[:, PC_N1 + l * 8:PC_N1 + l * 8 + 8] = np.asarray(norm1_g[l]).reshape(8, 128).T
        p[:, PC_N2 + l * 8:PC_N2 + l * 8 + 8] = np.asarray(norm2_g[l]).reshape(8, 128).T
        p[:, PC_BG + l * 16:PC_BG + l * 16 + 16] = np.asarray(b_gate[l]).reshape(16, 128).T
        p[:, PC_PS + l * 4:PC_PS + l * 4 + 4] = np.asarray(pool_scale[l]).reshape(4, 128).T
    p[:, PC_FN:PC_FN + 8] = np.asarray(final_norm_g).reshape(8, 128).T
    return p


def make_in_maps(x, meta_tokens, norm1_g, w_in, b_gate, pool_mix, pool_scale, w_branch_pool,
                 w_branch_sb, w_out, norm2_g, w_ffn_in, w_ffn_out, final_norm_g):
    f = lambda a: np.ascontiguousarray(np.asarray(a, dtype=np.float32))
    x = f(x)
    shared = dict(
        metaT=f(np.asarray(meta_tokens, np.float32).T),
        params=host_params(norm1_g, norm2_g, final_norm_g, b_gate, pool_scale),
        consts=host_consts(),
        w_in=f(w_in), pool_mix=f(pool_mix), w_branch_pool=f(w_branch_pool), w_branch_sb=f(w_branch_sb),
        w_out=f(w_out), w_ffn_in=f(w_ffn_in), w_ffn_out=f(w_ffn_out),
    )
    maps = []
    for b in range(x.shape[0]):
        m = dict(shared)
        m["xT"] = f(x[b].T)
        maps.append(m)
    return maps


_NC_CACHE = {}


def kernel(x, meta_tokens, norm1_g, w_in, b_gate, pool_mix, pool_scale, w_branch_pool,
           w_branch_sb, w_out, norm2_g, w_ffn_in, w_ffn_out, final_norm_g):
    in_maps = make_in_maps(x, meta_tokens, norm1_g, w_in, b_gate, pool_mix, pool_scale, w_branch_pool,
                           w_branch_sb, w_out, norm2_g, w_ffn_in, w_ffn_out, final_norm_g)
    nc = build()
    res = run_bass_kernel_spmd(nc, in_maps, core_ids=list(range(8)))
    out = np.stack([np.ascontiguousarray(r["yT"].T) for r in res.results], axis=0)
    return out.astype(np.float32)
```
